# Optimizing a Trainium2 kernel written in Bass

```python
import math
import jax, jax.numpy as jnp
from jax import lax
import numpy as np

D_MODEL = 1024
BATCH = 8
SEQ = 2048
DEPTH = 1
DEC_BATCH = 128
DEC_SEQ = 8
PAST_LEN = 16384
PAGE_SIZE = 128

MIX_WIDTH = D_MODEL
RET_HEADS = 4
RET_DK = (MIX_WIDTH // 2) // RET_HEADS
RET_DV = RET_DK
RET_W = RET_HEADS * RET_DK
HG_HEADS = 4
HG_DK = (MIX_WIDTH - RET_W) // HG_HEADS
HG_DV = HG_DK
HG_W = HG_HEADS * HG_DK
IN_COLS = 4 * RET_W + 4 * HG_W
D_FF = 4 * D_MODEL
CHUNK = 64
ROPE_BASE = 10000.0
NORM_EPS = 1e-6

kernel_name = "retnet_hgrn2_parallel_heads_step"


def rmsnorm(x, w):
    xf = x.astype(jnp.float32)
    r = xf * lax.rsqrt(jnp.mean(xf * xf, axis=-1, keepdims=True) + NORM_EPS)
    return (r * w.astype(jnp.float32)).astype(x.dtype)


def head_rmsnorm(o, w):
    return o * lax.rsqrt(jnp.mean(o * o, axis=-1, keepdims=True) + NORM_EPS) * w.astype(jnp.float32)


def rotary(x, pos):
    half = x.shape[-1] // 2
    inv_freq = ROPE_BASE ** (-jnp.arange(half, dtype=jnp.float32) / half)
    ang = pos.astype(jnp.float32)[:, None] * inv_freq[None, :]
    cos = jnp.cos(ang)[None, :, None, :]
    sin = jnp.sin(ang)[None, :, None, :]
    xf = x.astype(jnp.float32)
    x1, x2 = xf[..., :half], xf[..., half:]
    return jnp.concatenate([x1 * cos - x2 * sin, x1 * sin + x2 * cos], axis=-1)


def chunk_gla(q, k, v, log_g, s0):
    B, T, H, dk = q.shape
    dv = v.shape[-1]
    C = math.gcd(T, CHUNK)
    n = T // C

    def to_chunks(a):
        return a.astype(jnp.float32).reshape(B, n, C, H, a.shape[-1]).transpose(1, 0, 3, 2, 4)

    qc, kc, vc, gc = to_chunks(q), to_chunks(k), to_chunks(v), to_chunks(log_g)
    causal = jnp.tril(jnp.ones((C, C), dtype=bool))
    scalar_decay = log_g.shape[-1] == 1

    def step(S, inp):
        qb, kb, vb, gb = inp
        b = jnp.cumsum(gb, axis=2)
        b_last = b[:, :, -1:, :]
        if scalar_decay:
            diff = b[:, :, :, None, 0] - b[:, :, None, :, 0]
            decay = jnp.exp(jnp.where(causal, diff, -jnp.inf))
            A = jnp.einsum('bhtd,bhsd->bhts', qb, kb) * decay
        else:
            diff = b[:, :, :, None, :] - b[:, :, None, :, :]
            decay = jnp.exp(jnp.where(causal[:, :, None], diff, -jnp.inf))
            A = jnp.einsum('bhtd,bhsd,bhtsd->bhts', qb, kb, decay)
        o = jnp.einsum('bhts,bhsv->bhtv', A, vb) + jnp.einsum('bhtd,bhdv->bhtv', qb * jnp.exp(b), S)
        k_dec = kb * jnp.exp(b_last - b)
        S_new = S * jnp.exp(b_last[:, :, 0, :])[..., None] + jnp.einsum('bhsd,bhsv->bhdv', k_dec, vb)
        return S_new, o

    S_fin, oc = lax.scan(step, s0.astype(jnp.float32), (qc, kc, vc, gc))
    o = oc.transpose(1, 0, 3, 2, 4).reshape(B, T, H, dv)
    return o, S_fin


def mixer(h, pos, s_ret, s_hg, w_in, ret_norm_w, hgrn_norm_w, lb, w_out):
    B, T, _ = h.shape
    proj = h @ w_in
    sizes = [RET_W] * 4 + [HG_W] * 4
    idx = [int(s) for s in np.cumsum(sizes)[:-1]]
    rq, rk, rv, rg, hq, hf, hi, hg = jnp.split(proj, idx, axis=-1)

    rq = rotary(rq.reshape(B, T, RET_HEADS, RET_DK), pos)
    rk = rotary(rk.reshape(B, T, RET_HEADS, RET_DK), pos) * (RET_DK ** -0.5)
    rv = rv.reshape(B, T, RET_HEADS, RET_DV)
    log_gamma = jnp.log(1.0 - 2.0 ** (-5.0 - jnp.arange(RET_HEADS, dtype=jnp.float32)))
    log_g_ret = jnp.broadcast_to(log_gamma[None, None, :, None], (B, T, RET_HEADS, 1))
    o_r, s_ret_new = chunk_gla(rq, rk, rv, log_g_ret, s_ret)
    gate_r = jax.nn.silu(rg.astype(jnp.float32)).reshape(B, T, RET_HEADS, RET_DV)
    o_r = (head_rmsnorm(o_r, ret_norm_w) * gate_r).reshape(B, T, RET_W)

    hq = jax.nn.silu(hq.astype(jnp.float32)).reshape(B, T, HG_HEADS, HG_DK) * (HG_DK ** -0.5)
    lb_h = lb.reshape(HG_HEADS, HG_DK)
    f = lb_h + (1.0 - lb_h) * jax.nn.sigmoid(hf.astype(jnp.float32).reshape(B, T, HG_HEADS, HG_DK))
    log_f = jnp.log(f)
    k_in = 1.0 - f
    v_in = hi.reshape(B, T, HG_HEADS, HG_DV)
    o_h, s_hg_new = chunk_gla(hq, k_in, v_in, log_f, s_hg)
    gate_h = jax.nn.silu(hg.astype(jnp.float32)).reshape(B, T, HG_HEADS, HG_DV)
    o_h = (head_rmsnorm(o_h, hgrn_norm_w) * gate_h).reshape(B, T, HG_W)

    o = jnp.concatenate([o_r, o_h], axis=-1).astype(h.dtype)
    return o @ w_out, s_ret_new, s_hg_new


def setup_inputs(seed: int = 0) -> dict:
    key = jax.random.key(seed)
    ks = jax.random.split(key, 16)
    f32 = jnp.float32
    nrm = lambda k, shape, s: jax.random.normal(k, shape, f32) * s
    return {
        "x_prompt": nrm(ks[0], (BATCH, SEQ, D_MODEL), 1.0),
        "x_sample": nrm(ks[1], (DEC_BATCH, DEC_SEQ, D_MODEL), 1.0),
        "state_ret": nrm(ks[2], (DEPTH, DEC_BATCH, RET_HEADS, RET_DK, RET_DV), 0.3),
        "state_hgrn": nrm(ks[3], (DEPTH, DEC_BATCH, HG_HEADS, HG_DK, HG_DV), 0.3),
        "norm_mix_w": 1.0 + nrm(ks[4], (DEPTH, D_MODEL), 0.02),
        "w_in": nrm(ks[5], (DEPTH, D_MODEL, IN_COLS), D_MODEL ** -0.5),
        "ret_norm_w": 1.0 + nrm(ks[6], (DEPTH, RET_DV), 0.02),
        "hgrn_norm_w": 1.0 + nrm(ks[7], (DEPTH, HG_DV), 0.02),
        "lb_logits": nrm(ks[8], (DEPTH + 1, HG_W), 0.5),
        "w_out": nrm(ks[9], (DEPTH, MIX_WIDTH, D_MODEL), MIX_WIDTH ** -0.5),
        "norm_ffn_w": 1.0 + nrm(ks[10], (DEPTH, D_MODEL), 0.02),
        "w_up": nrm(ks[11], (DEPTH, D_MODEL, D_FF), D_MODEL ** -0.5),
        "w_down": nrm(ks[12], (DEPTH, D_FF, D_MODEL), D_FF ** -0.5),
        "final_norm_w": 1.0 + nrm(ks[13], (D_MODEL,), 0.02),
    }


def reference(x_prompt, x_sample, state_ret, state_hgrn, norm_mix_w, w_in, ret_norm_w, hgrn_norm_w,
              lb_logits, w_out, norm_ffn_w, w_up, w_down, final_norm_w):
    lb_all = jnp.cumsum(jax.nn.softmax(lb_logits.astype(jnp.float32), axis=0), axis=0)
    pos_p = jnp.arange(SEQ, dtype=jnp.int32)
    pos_s = PAST_LEN + jnp.arange(DEC_SEQ, dtype=jnp.int32)
    zero_ret = jnp.zeros((BATCH, RET_HEADS, RET_DK, RET_DV), jnp.float32)
    zero_hg = jnp.zeros((BATCH, HG_HEADS, HG_DK, HG_DV), jnp.float32)

    xp, xs = x_prompt, x_sample
    ret_p, hg_p, ret_s, hg_s = [], [], [], []
    for l in range(DEPTH):
        hp = rmsnorm(xp, norm_mix_w[l])
        mp, sr_p, sh_p = mixer(hp, pos_p, zero_ret, zero_hg, w_in[l], ret_norm_w[l], hgrn_norm_w[l], lb_all[l], w_out[l])
        hs = rmsnorm(xs, norm_mix_w[l])
        ms, sr_s, sh_s = mixer(hs, pos_s, state_ret[l], state_hgrn[l], w_in[l], ret_norm_w[l], hgrn_norm_w[l], lb_all[l], w_out[l])
        xp = xp + mp
        xs = xs + ms
        hp = rmsnorm(xp, norm_ffn_w[l])
        xp = xp + jnp.square(jax.nn.relu(hp @ w_up[l])) @ w_down[l]
        hs = rmsnorm(xs, norm_ffn_w[l])
        xs = xs + jnp.square(jax.nn.relu(hs @ w_up[l])) @ w_down[l]
        ret_p.append(sr_p.astype(x_prompt.dtype))
        hg_p.append(sh_p.astype(x_prompt.dtype))
        ret_s.append(sr_s.astype(state_ret.dtype))
        hg_s.append(sh_s.astype(state_hgrn.dtype))

    y_prompt = rmsnorm(xp, final_norm_w)
    y_sample = rmsnorm(xs, final_norm_w)
    ret_state_prompt = jnp.stack(ret_p, axis=0)
    hgrn_state_prompt = jnp.stack(hg_p, axis=0)
    ret_state_sample = jnp.stack(ret_s, axis=0)
    hgrn_state_sample = jnp.stack(hg_s, axis=0)
    return (y_prompt, y_sample, ret_state_prompt, hgrn_state_prompt, ret_state_sample, hgrn_state_sample)
```

```python
import numpy as np
import ml_dtypes
from contextlib import ExitStack

import concourse.bass as bass
import concourse.mybir as mybir
from concourse.bass_utils import run_bass_kernel_spmd

F32 = mybir.dt.float32
BF16 = mybir.dt.bfloat16
U8 = mybir.dt.uint8
AF = mybir.ActivationFunctionType
ALU = mybir.AluOpType

NCORES = 8
NT = 17
D = 1024
DFF = 4096
EPS = 1e-6
SCALE = 128.0 ** -0.5
GAM = [1.0 - 2.0 ** (-5.0 - h) for h in range(4)]


class _Op:
    __slots__ = ("id", "eng", "fn", "deps", "is_dma", "slot", "slot_count", "needs_inc", "sem_idx", "sem_val",
                 "cost", "nbytes", "seg", "fin", "aset")


class Sched:
    ENGS = ("pe", "act", "dve", "pool", "sp")
    DMAQ = ("sp", "pool")

    def __init__(self, nc, n_dma_sems=16, epoch=3000):
        self.nc = nc
        self.ops = []
        self.last_writer = {}
        self.readers = {}
        self.part_writer = {}
        self.part_readers = {}
        self.n_dma_sems = n_dma_sems
        self.epoch = epoch
        self.seg = 0
        self.op_pbs = {}
        self.first_of = {}
        self.pbs = []
        self.window = 10
        self.filler_fn = None
        self.filler_deps = []
        import os
        self.use_blevel = os.environ.get("SCHED_BLEVEL", "1") == "1"

    def _new(self, eng, fn, is_dma, cost, nbytes=0):
        op = _Op()
        op.id = len(self.ops)
        op.eng = eng
        op.fn = fn
        op.is_dma = is_dma
        op.needs_inc = False
        op.slot = None
        op.slot_count = 0
        op.sem_idx = 0
        op.sem_val = 0
        op.deps = []
        op.cost = cost
        op.nbytes = nbytes
        op.seg = self.seg
        op.fin = 0.0
        op.aset = None
        return op

    def _add(self, eng, fn, reads, writes, is_dma, cost, nbytes=0):
        op = self._new(eng, fn, is_dma, cost, nbytes)
        deps = set()

        def is_part(k_):
            return isinstance(k_, tuple) and len(k_) == 3 and k_[0] == "PART"

        def is_after(k_):
            return isinstance(k_, tuple) and len(k_) == 2 and k_[0] == "AFTER"

        for r in reads:
            if is_after(r):
                k2 = r[1]
                w = self.last_writer.get(k2)
                if w is not None:
                    deps.add(w)
                for w in self.part_writer.get(k2, {}).values():
                    deps.add(w)
                for rd in self.readers.get(k2, ()):
                    deps.add(rd)
                for lst in self.part_readers.get(k2, {}).values():
                    for rd in lst:
                        deps.add(rd)
            elif is_part(r):
                base, part = r[1], r[2]
                w = self.last_writer.get(base)
                if w is not None:
                    deps.add(w)
                w = self.part_writer.get(base, {}).get(part)
                if w is not None:
                    deps.add(w)
            else:
                w = self.last_writer.get(r)
                if w is not None:
                    deps.add(w)
                for w in self.part_writer.get(r, {}).values():
                    deps.add(w)
        for w_ in writes:
            if is_part(w_):
                base, part = w_[1], w_[2]
                w = self.last_writer.get(base)
                if w is not None:
                    deps.add(w)
                w = self.part_writer.get(base, {}).get(part)
                if w is not None:
                    deps.add(w)
                for rd in self.readers.get(base, ()):
                    deps.add(rd)
                for rd in self.part_readers.get(base, {}).get(part, ()):
                    deps.add(rd)
            else:
                w = self.last_writer.get(w_)
                if w is not None:
                    deps.add(w)
                for w in self.part_writer.get(w_, {}).values():
                    deps.add(w)
                for rd in self.readers.get(w_, ()):
                    deps.add(rd)
                for lst in self.part_readers.get(w_, {}).values():
                    for rd in lst:
                        deps.add(rd)
        deps.discard(op.id)
        op.deps = sorted(deps)
        for r in reads:
            if is_after(r):
                continue
            if is_part(r):
                self.part_readers.setdefault(r[1], {}).setdefault(r[2], []).append(op.id)
            else:
                self.readers.setdefault(r, []).append(op.id)
        for w_ in writes:
            if is_part(w_):
                self.part_writer.setdefault(w_[1], {})[w_[2]] = op.id
                self.part_readers.setdefault(w_[1], {})[w_[2]] = []
            else:
                self.last_writer[w_] = op.id
                self.readers[w_] = []
                self.part_writer[w_] = {}
                self.part_readers[w_] = {}
        self.ops.append(op)
        for kk in list(reads) + list(writes):
            if isinstance(kk, tuple) and len(kk) == 2 and kk[0] == "ps" and isinstance(kk[1], PB):
                pb = kk[1]
                if op.id not in pb.ops:
                    pb.ops.append(op.id)
                    self.op_pbs.setdefault(op.id, []).append(pb)
                if pb.first is None:
                    pb.first = op.id
                    pb.seg = self.seg
                    self.first_of[op.id] = pb
        return op.id

    def compute(self, eng, fn, reads=(), writes=(), cost=0.3, aset=None):
        oid = self._add(eng, fn, tuple(reads), tuple(writes), False, cost)
        self.ops[oid].aset = aset
        return oid

    def dma(self, queue, fn, reads=(), writes=(), nbytes=0):
        return self._add(queue, fn, tuple(reads), tuple(writes), True, 0.1 if queue == "sp" else 1.0, nbytes)

    def final_wait(self, eng, dep_ids):
        op = self._new(eng, None, False, 0.01)
        op.deps = sorted(set(dep_ids))
        self.ops.append(op)
        return op.id

    def barrier(self):
        self.seg += 1

    def schedule(self):
        import heapq
        ops = self.ops
        nseg = self.seg + 1
        order = {e: [] for e in self.ENGS}
        t_base = 0.0
        slot_last = {q: {} for q in self.DMAQ}
        slot_counts = {q: {} for q in self.DMAQ}
        dma_n = {q: 0 for q in self.DMAQ}
        succ = [[] for _ in ops]
        for op in ops:
            for d in op.deps:
                succ[d].append(op.id)
        self.seg_last = []
        for sg in range(nseg):
            seg_ops = [op for op in ops if op.seg == sg]
            if not seg_ops:
                self.seg_last.append([])
                continue
            indeg = {}
            ready_t = {}
            pend = {e: [] for e in self.ENGS}
            avail = {e: [] for e in self.ENGS}
            for op in seg_ops:
                n = 0
                rt = t_base
                for d in op.deps:
                    if ops[d].seg == sg:
                        n += 1
                indeg[op.id] = n
                ready_t[op.id] = rt
                if n == 0:
                    heapq.heappush(pend[op.eng], (rt, op.id))
            bl = {}
            for op in reversed(seg_ops):
                m = 0.0
                for s_ in succ[op.id]:
                    if ops[s_].seg == sg and s_ in bl:
                        m = max(m, bl[s_] + 0.2)
                xc = op.cost + (op.nbytes / 200e3 + 2.0 if op.is_dma else 0.0)
                bl[op.id] = m + xc
            PRI = self.use_blevel
            free = {e: t_base for e in self.ENGS}
            cur_set = [None]
            HOP = 0.2
            dma_pipe = t_base
            remaining = len(seg_ops)
            seg_end = t_base
            seg_order = {e: [] for e in self.ENGS}
            seg_pbs = sorted([pb for pb in self.pbs if pb.seg == sg and pb.first is not None], key=lambda p_: p_.first)
            for r_, pb in enumerate(seg_pbs):
                pb.idx = r_
            granted = set()
            fr_pos = [0]
            bank_occ = [None] * 8

            def frontier_idx():
                while fr_pos[0] < len(seg_pbs) and seg_pbs[fr_pos[0]].idx in granted:
                    fr_pos[0] += 1
                return seg_pbs[fr_pos[0]].idx if fr_pos[0] < len(seg_pbs) else 1 << 60

            def bank_for(pb, t):
                fidx = frontier_idx()
                if pb.idx > fidx + self.window:
                    return None
                nfree = 0
                bestb = None
                for b_ in range(8):
                    oc = bank_occ[b_]
                    if oc is None:
                        ft = t_base
                    elif oc.n_sched == len(oc.ops):
                        ft = oc.max_fin + 0.2
                    else:
                        continue
                    nfree += 1
                    if bestb is None or ft < bestb[1]:
                        bestb = (b_, ft)
                if bestb is None:
                    return None
                if pb.idx != fidx and nfree < 3:
                    return None
                return bestb

            while remaining:
                best = None
                for e in self.ENGS:
                    t = free[e]
                    while pend[e] and pend[e][0][0] <= t:
                        rt, oid = heapq.heappop(pend[e])
                        avail[e].append(oid)
                    keyf = (lambda o_: (-bl[o_], o_)) if PRI else (lambda o_: o_)
                    cands = []
                    if e == "pe":
                        pool_ = list(avail[e]) + [o_ for (_rt, o_) in pend[e]]
                        for o_ in pool_:
                            st_ = max(t, ready_t[o_])
                            pb_ = self.first_of.get(o_)
                            bsel = None
                            if pb_ is not None:
                                r_ = bank_for(pb_, st_)
                                if r_ is None:
                                    continue
                                bsel = r_[0]
                                st_ = max(st_, r_[1])
                            cands.append((st_, keyf(o_), o_, bsel))
                        if not cands:
                            continue
                        st_, _k, pick, bsel = min(cands, key=lambda c_: (c_[0], c_[1]))
                        cand = (st_, pick, e, pick in avail[e], bsel)
                    elif avail[e]:
                        pick = min(avail[e], key=keyf)
                        if e == "act" and cur_set[0] is not None:
                            same = [o_ for o_ in avail[e] if ops[o_].aset in (None, cur_set[0])]
                            if same:
                                pick = min(same, key=keyf)
                        cand = (t, pick, e, True, None)
                    elif pend[e]:
                        cand = (pend[e][0][0], pend[e][0][1], e, False, None)
                    else:
                        continue
                    if best is None or cand[:2] < best[:2]:
                        best = cand
                if best is None:
                    raise RuntimeError("scheduler deadlock")
                start, oid, e, from_avail, bsel = best
                if from_avail:
                    avail[e].remove(oid)
                else:
                    pend[e] = [x_ for x_ in pend[e] if x_[1] != oid]
                    heapq.heapify(pend[e])
                if bsel is not None:
                    pb_ = self.first_of[oid]
                    prev_occ = bank_occ[bsel]
                    if prev_occ is not None:
                        ops[oid].deps = sorted(set(ops[oid].deps) | set(prev_occ.ops))
                    bank_occ[bsel] = pb_
                    pb_.bank = bsel
                    granted.add(pb_.idx)
                op = ops[oid]
                if e == "pe" and self.filler_fn is not None and start - free[e] > 2.6 and free[e] > t_base:
                    tf = free[e] + 2.0
                    while tf < start - 0.4:
                        for _r in range(3):
                            fop = self._new("pe", self.filler_fn, False, 0.06)
                            fop.seg = sg
                            fop.deps = list(self.filler_deps)
                            fop.fin = tf + 0.06
                            self.ops.append(fop)
                            seg_order["pe"].append(fop.id)
                            tf += 0.06
                        tf += 2.0
                    ops = self.ops
                if e == "act" and op.aset is not None and op.aset != cur_set[0]:
                    start += 1.3
                    cur_set[0] = op.aset
                if op.is_dma:
                    q = op.eng
                    nsl = self.n_dma_sems if q == "sp" else 4
                    slot = dma_n[q] % nsl
                    dma_n[q] += 1
                    prev = slot_last[q].get(slot)
                    if prev is not None:
                        start = max(start, ops[prev].fin)
                        op.deps = sorted(set(op.deps) | {prev})
                    slot_last[q][slot] = oid
                    slot_counts[q][slot] = slot_counts[q].get(slot, 0) + 1
                    op.slot = slot
                    op.slot_count = slot_counts[q][slot]
                    issue_end = start + op.cost
                    xfer = op.nbytes / 200e3
                    begin = max(issue_end, dma_pipe)
                    dma_pipe = begin + xfer
                    op.fin = begin + xfer + 2.0
                    free[e] = issue_end
                else:
                    op.fin = start + op.cost
                    free[e] = op.fin
                seg_end = max(seg_end, op.fin)
                seg_order[e].append(oid)
                for pb_ in self.op_pbs.get(oid, ()):
                    pb_.n_sched += 1
                    pb_.max_fin = max(pb_.max_fin, op.fin)
                remaining -= 1
                for s_ in succ[oid]:
                    if ops[s_].seg != sg:
                        continue
                    ready_t[s_] = max(ready_t[s_], op.fin + (HOP if ops[s_].eng != op.eng else 0.05))
                    indeg[s_] -= 1
                    if indeg[s_] == 0:
                        heapq.heappush(pend[ops[s_].eng], (ready_t[s_], s_))
            lasts = []
            for e in self.ENGS:
                for oid in reversed(seg_order[e]):
                    if ops[oid].fn is not None and not ops[oid].is_dma:
                        lasts.append(oid)
                        break
                lasts.extend(oid for oid in seg_order[e] if ops[oid].is_dma)
            self.seg_last.append(lasts)
            for e in self.ENGS:
                if sg > 0:
                    bop = self._new(e, None, False, 0.01)
                    bop.seg = sg
                    bop.deps = sorted(set(self.seg_last[sg - 1]))
                    self.ops.append(bop)
                    order[e].append(bop.id)
                order[e].extend(seg_order[e])
            t_base = seg_end
            print(f"[sched] segment {sg}: {len(seg_ops)} ops, est end {seg_end:.1f} us")
        self.eng_order = order

    def emit(self):
        self.schedule()
        nc = self.nc
        ops = self.ops
        eng_ops = {e: [ops[i] for i in self.eng_order[e]] for e in self.ENGS}
        for e in self.ENGS:
            for op in eng_ops[e]:
                for d in op.deps:
                    dop = ops[d]
                    if dop.is_dma:
                        continue
                    if dop.eng == op.eng and op.eng == "pe":
                        continue
                    dop.needs_inc = True
        n_epochs = {}
        for e in self.ENGS:
            cnt = 0
            for op in eng_ops[e]:
                if op.is_dma or not op.needs_inc:
                    continue
                op.sem_idx = cnt // self.epoch
                op.sem_val = cnt % self.epoch + 1
                cnt += 1
            n_epochs[e] = max(1, (cnt + self.epoch - 1) // self.epoch)
        with ExitStack() as st:
            eng_sems = {e: [st.enter_context(nc.semaphore(f"s_{e}_{i}")) for i in range(n_epochs[e])]
                        for e in self.ENGS}
            dma_sems = {q: [st.enter_context(nc.semaphore(f"s_dma_{q}_{i}")) for i in range(self.n_dma_sems)]
                        for q in self.DMAQ}
            block = st.enter_context(nc.Block())

            def run_engine(ename, eng):
                waited = {}
                for op in eng_ops[ename]:
                    for d in op.deps:
                        dop = ops[d]
                        if dop.is_dma:
                            key = ("d", dop.eng, dop.slot)
                            sem = dma_sems[dop.eng][dop.slot]
                            val = 16 * dop.slot_count
                        else:
                            if dop.fn is None:
                                continue
                            if dop.eng == ename and ename == "pe":
                                continue
                            key = (dop.eng, dop.sem_idx)
                            sem = eng_sems[dop.eng][dop.sem_idx]
                            val = dop.sem_val
                        if waited.get(key, 0) >= val:
                            continue
                        waited[key] = val
                        eng.wait_ge(sem, val)
                    if op.fn is None:
                        continue
                    ins = op.fn(eng)
                    if op.is_dma:
                        ins.then_inc(dma_sems[op.eng][op.slot], 16)
                    elif op.needs_inc:
                        ins.then_inc(eng_sems[ename][op.sem_idx], 1)

            @block.tensor
            def _(eng):
                run_engine("pe", eng)

            @block.scalar
            def _(eng):
                run_engine("act", eng)

            @block.vector
            def _(eng):
                run_engine("dve", eng)

            @block.gpsimd
            def _(eng):
                run_engine("pool", eng)

            @block.sync
            def _(eng):
                run_engine("sp", eng)


ARENA_LOG = []


class PB:
    __slots__ = ("idx", "bank", "ops", "first", "n_sched", "max_fin", "seg")

    def __init__(self, idx):
        self.idx = idx
        self.bank = None
        self.ops = []
        self.first = None
        self.n_sched = 0
        self.max_fin = 0.0
        self.seg = 0


class LazyAP:
    def __init__(self, pb, base, chain=()):
        self.pb = pb
        self.base = base
        self.chain = chain

    def _ext(self, item):
        return LazyAP(self.pb, self.base, self.chain + (item,))

    def __getitem__(self, key):
        return self._ext(("getitem", key))

    def rearrange(self, pat, **kw):
        return self._ext(("rearrange", pat, kw))

    def bitcast(self, dt):
        return self._ext(("bitcast", dt))

    def resolve(self):
        ap = self.base[self.pb.bank if self.pb.bank is not None else 0]
        for it in self.chain:
            if it[0] == "getitem":
                ap = ap[it[1]]
            elif it[0] == "rearrange":
                ap = ap.rearrange(it[1], **it[2])
            else:
                ap = ap.bitcast(it[1])
        return ap

    @property
    def shape(self):
        return self.resolve().shape

    @property
    def dtype(self):
        return self.resolve().dtype


def RZ(x):
    return x.resolve() if isinstance(x, LazyAP) else x


class Arena:
    def __init__(self, ap_u8, nbytes):
        self.ap = ap_u8
        self.n = nbytes
        self.off = 0
        self.peak = 0

    def alloc(self, free_shape, dtype):
        esz = 4 if dtype == F32 else 2
        n = 1
        for s in free_shape:
            n *= s
        nb = (n * esz + 63) // 64 * 64
        assert self.off + nb <= self.n, f"arena overflow: need {self.off + nb} > {self.n}"
        v = self.ap[:, self.off:self.off + n * esz].bitcast(dtype)
        import inspect
        fr = inspect.stack()[1]
        ARENA_LOG.append((self.off, n * esz, "f32" if dtype == F32 else "bf16", fr.lineno, (fr.code_context or [""])[0].strip()[:60]))
        self.off += nb
        self.peak = max(self.peak, self.off)
        if len(free_shape) > 1:
            names = [f"a{i}" for i in range(len(free_shape))]
            v = v.rearrange("p (" + " ".join(names) + ") -> p " + " ".join(names),
                            **{nm: s for nm, s in zip(names, free_shape)})
        return v


def build_nc(debug=False, window=60):
    nc = bass.Bass("TRN2", target_bir_lowering=False)

    def din(name, shape, dt=F32):
        return nc.dram_tensor(name, list(shape), dt, kind="ExternalInput").ap()

    def dout(name, shape, dt=F32):
        return nc.dram_tensor(name, list(shape), dt, kind="ExternalOutput").ap()

    x = din("x", [NT, 128, D])
    sret = din("sret", [16, 4, 128, 128])
    shg = din("shg", [16, 4, 128, 128])
    w_in = din("w_in", [D, 4096])
    w_out = din("w_out", [D, D])
    w_up = din("w_up", [D, DFF])
    w_down = din("w_down", [DFF, D])
    nmix_d = din("nmix", [128, D])
    nffn_d = din("nffn", [128, D])
    nfin_d = din("nfin", [128, D])
    rnw_d = din("rnw", [128, 512])
    hnw_d = din("hnw", [128, 512])
    lbl_d = din("lbl", [128, 2, 4])
    ident_d = din("ident", [128, 128], BF16)
    cs_d = din("cs", [128, 2, NT, 64])
    gtab_d = din("gtab", [128, 2, 4, 4])
    masks_d = din("masks", [128, 3, 128], BF16)
    scanm_d = din("scanm", [128, 2, 512])
    colmask_d = din("colmask", [128, 16, 128], BF16)
    rowmask_d = din("rowmask", [128, 16, 128], BF16)

    y = dout("y", [NT, 128, D])
    srp = dout("srp", [4, 128, 128])
    shp = dout("shp", [4, 128, 128])
    srs = dout("srs", [16, 4, 128, 128])
    shs = dout("shs", [16, 4, 128, 128])

    wu_s = nc.dram_tensor("wu_s", [8, 128, 8, 512], BF16, kind="Internal").ap()
    wd_s = nc.dram_tensor("wd_s", [8, 128, 4, 1024], BF16, kind="Internal").ap()
    wo_s = nc.dram_tensor("wo_s", [128, 8, 1024], BF16, kind="Internal").ap()

    S = Sched(nc)
    S.window = window
    NB = 207 * 1024
    with ExitStack() as ctx:
        arena_t = ctx.enter_context(nc.sbuf_tensor("arena", [128, NB], U8))
        ps = [ctx.enter_context(nc.psum_tensor(f"ps{k}", [128, 512], F32)) for k in range(8)]
        A = Arena(arena_t[:, :], NB)

        PSB = [ps[i_][:, :] for i_ in range(8)]

        def PS(k):
            return LazyAP(k, PSB)

        def PS4(k):
            return LazyAP(k, PSB).rearrange("p (h d) -> p h d", h=4)

        def PST(k):
            return LazyAP(k, PSB).bitcast(BF16).rearrange("p (k t) -> p k t", k=8)

        def fsz(ap):
            n = 1
            for d_ in ap.shape[1:]:
                n *= d_
            return n

        def DMA(q, out, in_, r=(), w=()):
            esz = 4 if out.dtype == F32 else 2
            nb = out.shape[0] * fsz(out) * esz
            return S.dma(q, lambda e: e.dma_start(out=out, in_=in_), r, w, nbytes=nb)

        def MM(out, lhsT, rhs, start, stop, r, w):
            S.compute("pe", lambda e: e.matmul(RZ(out), lhsT=lhsT, rhs=rhs, start=start, stop=stop), r, w,
                      cost=0.036 + fsz(rhs) * 0.00039)

        def TR(out, in_, r, w):
            S.compute("pe", lambda e: e.transpose(out=RZ(out), in_=in_, identity=ident), list(r) + ["ident"], w, cost=0.085)

        def ACT(out, in_, func, r, w, scale=1.0, bias=None, accum=None):
            def f(e):
                kw = {}
                if bias is not None:
                    kw["bias"] = bias
                if accum is not None:
                    kw["accum_out"] = accum
                return e.activation(out=RZ(out), in_=RZ(in_), func=func, scale=scale, **kw)
            aset = {AF.Silu: 18, AF.Tanh: 18, AF.Sigmoid: 2, AF.Ln: 6, AF.Exp: 6, AF.Sqrt: 3}.get(func)
            S.compute("act", f, r, w, cost=0.12 + fsz(in_) / 1200.0 + (0.1 if accum is not None else 0.0), aset=aset)

        def ecost(eng, ap):
            return (0.1 + fsz(ap) / 960.0) if eng == "dve" else (0.15 + fsz(ap) / 450.0)

        def TT(eng, out, in0, in1, op, r, w):
            S.compute(eng, lambda e: e.tensor_tensor(out=RZ(out), in0=RZ(in0), in1=RZ(in1), op=op), r, w, cost=ecost(eng, out))

        def TS(eng, out, in0, s1, s2, op0, op1, r, w):
            if s2 is None:
                S.compute(eng, lambda e: e.tensor_scalar(out=RZ(out), in0=RZ(in0), scalar1=s1, scalar2=None, op0=op0), r, w,
                          cost=ecost(eng, out))
            else:
                S.compute(eng, lambda e: e.tensor_scalar(out=RZ(out), in0=RZ(in0), scalar1=s1, scalar2=s2, op0=op0, op1=op1), r, w,
                          cost=ecost(eng, out))

        def STT(out, in0, scalar, in1, op0, op1, r, w):
            S.compute("dve", lambda e: e.scalar_tensor_tensor(out=RZ(out), in0=RZ(in0), scalar=scalar, in1=RZ(in1), op0=op0, op1=op1), r, w,
                      cost=ecost("dve", out))

        def CP(eng, out, in_, r, w):
            if eng == "act":
                ACT(out, in_, AF.Copy, r, w)
            else:
                S.compute(eng, lambda e: e.tensor_copy(out=RZ(out), in_=RZ(in_)), r, w, cost=ecost(eng, out))

        def RECIP(out, in_, r, w):
            S.compute("dve", lambda e: e.reciprocal(out=out, in_=in_), r, w, cost=ecost("dve", out))

        def MEMSET(eng, ap, val, w):
            S.compute(eng, lambda e: e.memset(ap, val), (), w, cost=ecost(eng, ap))

        def bc_mid(ap2, n):
            return ap2.unsqueeze(1).broadcast_to([128, n, ap2.shape[1]])

        def bc_last(ap2, n):
            return ap2.unsqueeze(2).broadcast_to([128, ap2.shape[1], n])

        held = set()
        alloc_n = [0]

        def newps():
            pb = PB(alloc_n[0])
            alloc_n[0] += 1
            assert len(held) < 8, "more than 8 live PSUM banks"
            held.add(pb)
            S.pbs.append(pb)
            return pb

        def rel(*ks):
            for k in ks:
                held.discard(k)

        ogT = A.alloc([8, NT * 128], BF16)
        ident = A.alloc([128], BF16)
        cs = A.alloc([2, NT, 64], F32)
        gtab = A.alloc([2, 4, 4], F32)
        masks = A.alloc([3, 128], BF16)
        scanm = A.alloc([2, 512], F32)
        rnw = A.alloc([512], F32)
        hnw = A.alloc([512], F32)
        lbt = A.alloc([2, 4], F32)
        lb = A.alloc([4], F32)
        oml = A.alloc([4], F32)
        homl = A.alloc([4], F32)
        lbh = A.alloc([4], F32)
        sm = A.alloc([64], F32)
        junk = A.alloc([128], BF16)
        epst = A.alloc([1], F32)
        win = A.alloc([8, 4096], BF16)
        mark_win = A.off - 8 * 4096 * 2
        mark = A.off

        ident_dma = DMA("sp", ident, ident_d, w=["ident"])
        import os as _os
        if _os.environ.get("NO_FILLERS", "0") != "1":
            S.filler_fn = lambda e: e.ldweights(ident)
            S.filler_deps = [ident_dma]
        DMA("sp", cs, cs_d, w=["cs"])
        DMA("sp", gtab, gtab_d, w=["gtab"])
        DMA("sp", masks, masks_d, w=["masks"])
        DMA("sp", scanm, scanm_d, w=["scanm"])
        DMA("sp", rnw, rnw_d, w=["rnw"])
        DMA("sp", hnw, hnw_d, w=["hnw"])
        DMA("sp", lbt, lbl_d, w=["lbt"])
        MEMSET("pool", epst, EPS, ["epst"])
        TT("dve", lb, lbt[:, 0, :], lbt[:, 1, :], ALU.subtract, ["lbt"], ["lb"])
        ACT(lb, lb, AF.Sigmoid, ["lb"], ["lb"])
        TS("dve", oml, lb, -1.0, 1.0, ALU.mult, ALU.add, ["lb"], ["oml"])
        TS("dve", homl, oml, 0.5, None, ALU.mult, None, ["oml"], ["homl"])
        TT("dve", lbh, lb, homl, ALU.add, ["lb", "homl"], ["lbh"])

        nmix = A.alloc([D], F32)
        Sr = A.alloc([4, 128], F32)
        Sbr = A.alloc([4, 128], BF16)
        Sh = A.alloc([4, 128], F32)
        Sbh = [A.alloc([4, 128], BF16) for _ in range(2)]
        mark_mix = A.off

        def alloc_tile_bufs():
            b = {}
            b["xt"] = A.alloc([D], F32)
            b["hbf"] = A.alloc([D], BF16)
            b["hT"] = A.alloc([8, 128], BF16)
            b["rt"] = [A.alloc([4, 64], F32) for _ in range(4)]
            b["qrot"] = A.alloc([4, 128], BF16)
            b["ktm"] = A.alloc([4, 128], BF16)
            b["vp"] = A.alloc([4, 128], BF16)
            b["gate"] = A.alloc([512], F32)
            b["qkT"] = A.alloc([8, 128], BF16)
            b["ATm"] = A.alloc([4, 128], BF16)
            b["og"] = A.alloc([4, 128], BF16)
            b["B"] = [A.alloc([512], F32) for _ in range(4)]
            b["qtT"] = A.alloc([4, 128], BF16)
            b["ktT"] = A.alloc([4, 128], BF16)
            b["ktmh"] = A.alloc([4, 128], BF16)
            b["vph"] = A.alloc([4, 128], BF16)
            b["gateh"] = A.alloc([512], F32)
            b["ogh"] = A.alloc([4, 128], BF16)
            b["ATmh"] = A.alloc([4, 128], BF16)
            return b

        TB = [alloc_tile_bufs() for _ in range(2)]
        NHT = 4
        HTX = [TB[0]["hT"], TB[1]["hT"]] + [A.alloc([8, 128], BF16) for _ in range(NHT - 2)]
        print("mixer arena bytes/partition:", A.off)

        w_in_v = w_in.rearrange("(kc p) n -> p kc n", p=128)
        prevk = []
        for (c0, n_) in ((0, 2), (2, 2), (6, 2)):
            DMA("pool", win[:, :, c0 * 512:(c0 + n_) * 512], w_in_v[:, :, c0 * 512:(c0 + n_) * 512], r=list(prevk),
                w=[("win", cb) for cb in range(c0, c0 + n_)])
            prevk = [("win", c0)]
        stg = arena_t[:, :].bitcast(F32)[:, 0:8 * 1024].rearrange("p (k n) -> p k n", k=8)
        DMA("sp", stg, w_in_v[:, :, 2048:3072], r=[("win", 0)], w=["stg"])
        for kc in range(8):
            S.compute("pool", lambda e, kc=kc: e.tensor_copy(out=win[:, kc, 2048:3072], in_=stg[:, kc, :]), ["stg"],
                      [("PART", ("win", 4), kc), ("PART", ("win", 5), kc)], cost=2.4)
        DMA("sp", nmix, nmix_d, w=["nmix"])
        w_up_v = w_up.rearrange("(kc p) n -> p kc n", p=128)
        for cb in range(8):
            DMA("pool", wu_s[cb], w_up_v[:, :, cb * 512:(cb + 1) * 512], r=[("win", 0), ("win", 4)], w=[("wu_s", cb)])
        DMA("pool", wo_s, w_out.rearrange("(kc p) n -> p kc n", p=128), r=[("win", 0), ("win", 4)], w=["wo_s"])
        w_dn_v = w_down.rearrange("(q f p) n -> q p f n", f=4, p=128)
        for q in range(8):
            DMA("pool", wd_s[q], w_dn_v[q], r=[("win", 0), ("win", 4)], w=[("wd_s", q)])

        MEMSET("pool", Sr, 0.0, ["Sr"])
        MEMSET("pool", Sh, 0.0, ["Sh"])
        MEMSET("pool", Sbr, 0.0, ["Sbr"])
        MEMSET("pool", Sbh[1], 0.0, [("Sbh", 1)])

        def rstd_from_ss(ss_ap, n, key):
            ACT(ss_ap, ss_ap, AF.Ln, [key, "epst"], [key], scale=1.0 / n, bias=epst[:, 0:1])
            ACT(ss_ap, ss_ap, AF.Exp, [key], [key], scale=-0.5)

        out_dmas = []
        sample_bufs = {}

        def mixer_tile(i, hchunk):
            kind = 0 if i < 16 else 1
            p = i % 2
            tb = TB[p]
            K = lambda nm: (nm, p)
            xb = tb["xt"]
            hbf, hT = tb["hbf"], HTX[i % NHT]
            rt, qrot, ktm, vp, gate = tb["rt"], tb["qrot"], tb["ktm"], tb["vp"], tb["gate"]
            qkT, ATm, og = tb["qkT"], tb["ATm"], tb["og"]
            B1, B2, B3, B4 = tb["B"]
            qtT, ktT, ktmh, vph, gateh, ogh = tb["qtT"], tb["ktT"], tb["ktmh"], tb["vph"], tb["gateh"], tb["ogh"]
            ss1 = sm[:, p:p + 1]
            DMA("sp", xb, x[i], w=[K("xt")])
            ACT(hbf, xb, AF.Square, [K("xt")], [K("hbf"), K("ss1")], accum=ss1)
            rstd_from_ss(ss1, D, K("ss1"))
            STT(hbf, xb, ss1, nmix, ALU.mult, ALU.mult, [K("xt"), K("ss1"), "nmix"], [K("hbf")])
            pT = newps()
            for kc in range(8):
                TR(PST(pT)[:, kc, :], hbf[:, kc * 128:(kc + 1) * 128], [K("hbf")], [("ps", pT)])
            CP("act", hT, PST(pT), [("ps", pT)], [("hT", i % NHT)])
            rel(pT)

            pr = {}
            for cb in (0, 1, 2, 3):
                pb = newps()
                pr[cb] = pb
                for kc in range(8):
                    MM(PS(pb), hT[:, kc, :], win[:, kc, cb * 512:(cb + 1) * 512], kc == 0, kc == 7,
                       [("hT", i % NHT), ("win", cb)], [("ps", pb)])
                if cb == 1:
                    pass
            cosb = bc_mid(cs[:, 0, i, :], 4)
            sinb = bc_mid(cs[:, 1, i, :], 4)
            for (bank, dst, dkey, sc) in ((pr[0], qrot, K("qrot"), None), (pr[1], ktm, K("ktm"), None)):
                pv = PS(bank).rearrange("p (h two d) -> p h two d", h=4, two=2)
                x1 = pv[:, :, 0, :]
                x2 = pv[:, :, 1, :]
                dv = dst.rearrange("p h (two d) -> p h two d", two=2)
                pk = ("ps", bank)

                def prod(o, a_, b_, okey):
                    if sc is None:
                        TT("dve", o, a_, b_, ALU.mult, [pk, "cs"], [okey])
                    else:
                        STT(o, a_, sc, b_, ALU.mult, ALU.mult, [pk, "cs"], [okey])
                prod(rt[0], x1, cosb, K("rt0"))
                prod(rt[1], x2, sinb, K("rt1"))
                TT("pool", dv[:, :, 0, :], rt[0], rt[1], ALU.subtract, [K("rt0"), K("rt1")], [dkey])
                prod(rt[2], x1, sinb, K("rt2"))
                prod(rt[3], x2, cosb, K("rt3"))
                TT("pool", dv[:, :, 1, :], rt[2], rt[3], ALU.add, [K("rt2"), K("rt3")], [dkey])
            rel(pr[0], pr[1])
            TT("dve", vp, PS4(pr[2]), bc_last(gtab[:, kind, 1, :], 128), ALU.mult, [("ps", pr[2]), "gtab"], [K("vp")])
            ACT(gate, PS(pr[3]), AF.Silu, [("ps", pr[3])], [K("gate")])
            rel(pr[2], pr[3])
            TT("pool", gate, gate, rnw, ALU.mult, [K("gate"), "rnw"], [K("gate")])
            pq = newps()
            for h in range(4):
                TR(PST(pq)[:, h, :], qrot[:, h, :], [K("qrot")], [("ps", pq)])
            for h in range(4):
                TR(PST(pq)[:, 4 + h, :], ktm[:, h, :], [K("ktm")], [("ps", pq)])
            CP("dve", qkT, PST(pq), [("ps", pq)], [K("qkT")])
            rel(pq)
            pa = newps()
            for h in range(4):
                MM(PS4(pa)[:, h, :], qkT[:, 4 + h, :], qkT[:, h, :], True, True, [K("qkT")], [("ps", pa)])
            mk = 0 if kind == 0 else 2
            TT("dve", ATm, PS4(pa), bc_mid(masks[:, mk, :], 4), ALU.mult, [("ps", pa), "masks"], [K("ATm")])
            rel(pa)
            B1h = B1.rearrange("p (h t) -> p h t", h=4)
            B2h = B2.rearrange("p (h t) -> p h t", h=4)
            B3h = B3.rearrange("p (h t) -> p h t", h=4)
            ATh = tb["ATmh"]

            def sample_state(grp, h, src_q, src_v, kt_ap, state_in, state_out, evec, qkey, vkey, ktkey, po_):
                hh = grp * 4 + h
                sb = hh % 4
                sb2 = hh % 2
                S0f, S0b_ = sample_bufs["S0f"], sample_bufs["S0b"]
                Qexp, Vexp = sample_bufs["Qexp"], sample_bufs["Vexp"][sb2]
                vxk = ("Vexp", sb2)
                rowmask = sample_bufs["rowmask"]
                sfk = ("S0f", sb)
                sbk = ("S0b", sb2)
                S0b = {sb: S0b_[sb2]}
                DMA("sp", S0f[sb], state_in[:, h, :, :].rearrange("j p v -> p j v"), r=["winalias"], w=[sfk])
                CP("act", S0b[sb], S0f[sb], [sfk, "winalias"], [sbk])
                CP("dve", sample_bufs["Qdiag"], src_q.rearrange("p (j e) -> p j e", e=8), [qkey, "winalias"], ["Qexp"])
                for j in range(16):
                    MM(PS4(po_)[:, h, :], Qexp[:, j, :], S0b[sb][:, j, :], False, j == 15, ["Qexp", sbk], [("ps", po_)])
                TT("pool", Vexp, rowmask, bc_mid(src_v, 16), ALU.mult, ["rowmask", vkey, "winalias"], [vxk])
                pus = [newps() for _ in range(4)]
                for b_ in range(4):
                    MM(PS(pus[b_]), kt_ap, Vexp[:, 4 * b_:4 * b_ + 4, :], True, True, [vxk, ktkey], [("ps", pus[b_])])
                for b_ in range(4):
                    TT("dve", S0f[sb][:, 4 * b_:4 * b_ + 4, :], S0f[sb][:, 4 * b_:4 * b_ + 4, :], PS4(pus[b_]), ALU.add,
                       [sfk, ("ps", pus[b_])], [sfk])
                rel(*pus)
                if evec is None:
                    ACT(S0f[sb], S0f[sb], AF.Copy, [sfk], [sfk], scale=GAM[h] ** 8)
                else:
                    TT("dve", S0f[sb], S0f[sb], bc_last(evec, 128), ALU.mult, [sfk, K("B1")], [sfk])
                return DMA("sp", state_out[:, h, :, :].rearrange("j p v -> p j v"), S0f[sb], r=[sfk])

            def ret_rec():
              po = newps()
              for h in range(4):
                MM(PS4(po)[:, h, :], ATm[:, h, :], vp[:, h, :], True, False, [K("ATm"), K("vp")], [("ps", po)])
                if kind == 0:
                    MM(PS4(po)[:, h, :], qkT[:, h, :], Sbr[:, h, :], False, True, [K("qkT"), "Sbr"], [("ps", po)])
                else:
                    out_dmas.append(sample_state(0, h, qkT[:, h, :], vp[:, h, :], ktm[:, h, :], sret, srs, None,
                                                 K("qkT"), K("vp"), K("ktm"), po))
              if kind == 0:
                pu = newps()
                for h in range(4):
                    MM(PS4(pu)[:, h, :], ktm[:, h, :], vp[:, h, :], True, True, [K("ktm"), K("vp")], [("ps", pu)])
                TT("dve", Sr, Sr, PS4(pu), ALU.add, ["Sr", ("ps", pu)], ["Sr"])
                rel(pu)
                TT("dve", Sbr, Sr, bc_last(gtab[:, 0, 2, :], 128), ALU.mult, ["Sr", "gtab"], ["Sbr"])
                TT("pool", Sr, Sr, bc_last(gtab[:, 0, 2, :], 128), ALU.mult, ["Sr", "gtab"], ["Sr"])
                if i == 15:
                    out_dmas.append(DMA("sp", srp.rearrange("h p v -> p h v"), Sr, r=["Sr"]))
              out_norm(po, gate, K("gate"), og, K("og"), gtab[:, kind, 0, :], 8 + 8 * p, gtab[:, kind, 3, :])
              rel(po)
              pt2 = newps()
              for h in range(4):
                TR(PST(pt2)[:, h, :], og[:, h, :], [K("og")], [("ps", pt2)])
              CP("act", ogT[:, 0:4, i * 128:(i + 1) * 128], PST(pt2)[:, 0:4, :], [("ps", pt2), ("AFTER", "stg")], [("ogTa", i)])
              rel(pt2)

            def out_norm(po_, gate_, gkey, og_, ogkey, gscale, c0, gscale2=None):
                ssk = ("ss4", c0)
                ACT(og_, PS4(po_), AF.Square, [("ps", po_)], [ogkey])
                S.compute("dve", lambda e: e.tensor_reduce(out=sm[:, c0:c0 + 4], in_=og_, axis=mybir.AxisListType.X, op=ALU.add),
                          [ogkey], [ssk], cost=0.65)
                if gscale is not None:
                    TT("dve", sm[:, c0:c0 + 4], sm[:, c0:c0 + 4], gscale2, ALU.mult, [ssk, "gtab"], [ssk])
                rstd_from_ss(sm[:, c0:c0 + 4], 128, ssk)
                if gscale is not None:
                    TT("dve", sm[:, c0:c0 + 4], sm[:, c0:c0 + 4], gscale, ALU.mult, [ssk, "gtab"], [ssk])
                for h in range(4):
                    STT(og_[:, h, :], PS4(po_)[:, h, :], sm[:, c0 + h:c0 + h + 1], gate_[:, h * 128:(h + 1) * 128],
                        ALU.mult, ALU.mult, [("ps", po_), ssk, gkey], [ogkey])

            def hg_prep():
              for cb in (6, 7):
                  pb = newps()
                  pr[cb] = pb
                  for kc in range(8):
                      MM(PS(pb), hT[:, kc, :], win[:, kc, cb * 512:(cb + 1) * 512], kc == 0, kc == 7,
                         [("hT", i % NHT), ("win", cb)], [("ps", pb)])
              for cb in (4, 5):
                  pb = newps()
                  pr[cb] = pb
                  for h in range(4):
                      for kc in range(8):
                          MM(PS4(pb)[:, h, :], win[:, kc, cb * 512 + h * 128:cb * 512 + (h + 1) * 128], hT[:, kc, :],
                             kc == 0, kc == 7, [("hT", i % NHT), ("win", cb)], [("ps", pb)])
              ACT(B1, PS(pr[5]), AF.Tanh, [("ps", pr[5])], [K("B1")], scale=0.5)
              rel(pr[5])
              TS("dve", B3, B1, -1.0, 1.0, ALU.mult, ALU.add, [K("B1")], [K("B3")])
              for h in range(4):
                  ACT(B1h[:, h, :], B1h[:, h, :], AF.Ln, [K("B1"), "lbh", "homl"], [K("B1")], scale=homl[:, h:h + 1], bias=lbh[:, h:h + 1])
              S.compute("dve", lambda e, k_=kind: e.tensor_tensor_scan(out=B2, data0=scanm[:, k_, :], data1=B1, initial=0.0,
                                                                       op0=ALU.mult, op1=ALU.add),
                        [K("B1"), "scanm"], [K("B2")], cost=1.2)
              ACT(B1, B2, AF.Exp, [K("B2")], [K("B1")])
              ACT(B2, B2, AF.Exp, [K("B2")], [K("B2")], scale=-1.0)
              ACT(B4, PS(pr[4]), AF.Silu, [("ps", pr[4])], [K("B4")])
              rel(pr[4])
              STT(qtT.rearrange("p h t -> p (h t)"), B4, SCALE, B1, ALU.mult, ALU.mult, [K("B4"), K("B1")], [K("qtT")])
              for h in range(4):
                  STT(ktT[:, h, :], B3h[:, h, :], homl[:, h:h + 1], B2h[:, h, :], ALU.mult, ALU.mult,
                      [K("B3"), K("B2"), "homl"], [("PART", K("ktT"), h)])
              pk2 = newps()
              for h in range(4):
                  TR(PST(pk2)[:, h, :], ktT[:, h, :], [K("ktT")], [("ps", pk2)])
              CP("dve", ktmh, PST(pk2)[:, 0:4, :], [("ps", pk2)], [K("ktmh")])
              CP("act", vph, PS4(pr[6]), [("ps", pr[6])], [K("vph")])
              ACT(gateh, PS(pr[7]), AF.Silu, [("ps", pr[7])], [K("gateh")])
              rel(pk2, pr[6], pr[7])
              TT("pool", gateh, gateh, hnw, ALU.mult, [K("gateh"), "hnw"], [K("gateh")])
              pa2 = newps()
              for h in range(4):
                  MM(PS4(pa2)[:, h, :], ktT[:, h, :], qtT[:, h, :], True, True, [K("ktT"), K("qtT")], [("ps", pa2)])
              mk = 1 if kind == 0 else 2
              TT("dve", ATh, PS4(pa2), bc_mid(masks[:, mk, :], 4), ALU.mult, [("ps", pa2), "masks"], [K("ATmh")])
              rel(pa2)

            def hg_rec():
              po2 = newps()
              if kind == 0:
                  ca = hchunk
                  prev = (ca + 1) % 2
                  cur = ca % 2
                  pu1 = newps()
                  for h in range(4):
                      MM(PS4(pu1)[:, h, :], ktmh[0:64, h, :], vph[0:64, h, :], True, True, [K("ktmh"), K("vph")], [("ps", pu1)])
                  TT("dve", Sh, Sh, PS4(pu1), ALU.add, ["Sh", ("ps", pu1)], ["Sh"])
                  rel(pu1)
                  TT("dve", Sbh[cur], Sh, B1h[:, :, 63:64].broadcast_to([128, 4, 128]), ALU.mult, ["Sh", K("B1")], [("Sbh", cur)])
                  TT("pool", Sh, Sh, B1h[:, :, 63:64].broadcast_to([128, 4, 128]), ALU.mult, ["Sh", K("B1")], ["Sh"])
                  for h in range(4):
                      MM(PS4(po2)[:, h, :], ATh[:, h, :], vph[:, h, :], True, False, [K("ATmh"), K("vph")], [("ps", po2)])
                      MM(PS4(po2)[0:64, h, :], qtT[:, h, 0:64], Sbh[prev][:, h, :], False, True, [K("qtT"), ("Sbh", prev)], [("ps", po2)])
                      MM(PS4(po2)[64:128, h, :], qtT[:, h, 64:128], Sbh[cur][:, h, :], False, True, [K("qtT"), ("Sbh", cur)], [("ps", po2)])
                  pu2 = newps()
                  for h in range(4):
                      MM(PS4(pu2)[:, h, :], ktmh[64:128, h, :], vph[64:128, h, :], True, True, [K("ktmh"), K("vph")], [("ps", pu2)])
                  TT("dve", Sh, Sh, PS4(pu2), ALU.add, ["Sh", ("ps", pu2)], ["Sh"])
                  rel(pu2)
                  TT("dve", Sbh[prev], Sh, B1h[:, :, 127:128].broadcast_to([128, 4, 128]), ALU.mult, ["Sh", K("B1")], [("Sbh", prev)])
                  TT("pool", Sh, Sh, B1h[:, :, 127:128].broadcast_to([128, 4, 128]), ALU.mult, ["Sh", K("B1")], ["Sh"])
                  if i == 15:
                      out_dmas.append(DMA("sp", shp.rearrange("h p v -> p h v"), Sh, r=["Sh"]))
              else:
                  for h in range(4):
                      MM(PS4(po2)[:, h, :], ATh[:, h, :], vph[:, h, :], True, False, [K("ATmh"), K("vph")], [("ps", po2)])
                      evec = B1h[:, h, :].rearrange("p (j e) -> p j e", e=8)[:, :, 7]
                      out_dmas.append(sample_state(1, h, qtT[:, h, :], vph[:, h, :], ktmh[:, h, :], shg, shs, evec,
                                                   K("qtT"), K("vph"), K("ktmh"), po2))
              out_norm(po2, gateh, K("gateh"), ogh, K("ogh"), None, 24 + 8 * p)
              rel(po2)
              pt3 = newps()
              for h in range(4):
                  TR(PST(pt3)[:, h, :], ogh[:, h, :], [K("ogh")], [("ps", pt3)])
              CP("act", ogT[:, 4:8, i * 128:(i + 1) * 128], PST(pt3)[:, 0:4, :], [("ps", pt3), ("AFTER", "stg")], [("ogTb", i)])
              rel(pt3)

            if kind == 0:
                ret_rec()
                hg_prep()
                hg_rec()
            else:
                hg_prep()
                S.compute("pe", lambda e: e.ldweights(ident), ["ident"], [("win", cb_) for cb_ in range(8)] + ["winalias"], cost=0.05)
                DMA("sp", sample_bufs["rowmask"], rowmask_d, r=["winalias"], w=["rowmask"])
                S.compute("pool", lambda e: e.memset(sample_bufs["Qexp"], 0.0), ["winalias"], ["Qexp"], cost=4.0)
                ret_rec()
                hg_rec()

        hchunk = 0
        import os
        DBG = int(os.environ.get("DBG_TILES", "0"))
        for i in range(DBG if DBG else 16):
            mixer_tile(i, hchunk)
            hchunk += 2
        if DBG:
            S.final_wait("sp", out_dmas)
            S.emit()
            return nc
        A2 = Arena(arena_t[:, :], NB)
        A2.off = mark_win
        sample_bufs["rowmask"] = A2.alloc([16, 128], BF16)
        sample_bufs["Qexp"] = A2.alloc([16, 128], BF16)
        sample_bufs["Vexp"] = [A2.alloc([16, 128], BF16) for _ in range(2)]
        sample_bufs["S0f"] = [A2.alloc([16, 128], F32) for _ in range(4)]
        sample_bufs["S0b"] = [A2.alloc([16, 128], BF16) for _ in range(2)]
        assert A2.off <= mark_win + 8 * 4096 * 2
        qx = sample_bufs["Qexp"]
        sample_bufs["Qdiag"] = bass.AP(qx.tensor, qx.offset, [[qx.ap[0][0], 128], [136, 16], [1, 8]])
        mixer_tile(16, hchunk)
        S.barrier()

        A.off = mark_win
        aT = A.alloc([32, 512], BF16)
        Xbs = [A.alloc([4, D], F32) for _ in range(2)]
        h2Ts = [A.alloc([8, 512], BF16) for _ in range(2)]
        nffn = A.alloc([D], F32)
        nfin = A.alloc([D], F32)
        wout = A.alloc([8, D], BF16)
        wupb = [A.alloc([8, 512], BF16) for _ in range(2)]
        wdnb = [A.alloc([4, D], BF16) for _ in range(2)]
        wdn4 = [wdnb[j_ // 2][:, :, (j_ % 2) * 512:(j_ % 2 + 1) * 512] for j_ in range(4)]
        h2s = [A.alloc([D], BF16) for _ in range(2)]
        rl = [A.alloc([512], F32) for _ in range(2)]
        print("post arena bytes/partition:", A.off)

        DMA("sp", nffn, nffn_d, w=["nffn"])
        DMA("sp", nfin, nfin_d, w=["nfin"])
        DMA("sp", wout, wo_s, r=["wo_s"], w=["wout"])

        blocks = [[0, 1, 2, 3], [4, 5, 6], [7, 8, 9], [10, 11, 12], [13, 14, 15, 16]]
        ydmas = []

        def prologue(bi):
            blk = blocks[bi]
            bp = bi % 2
            Xb = Xbs[bp]
            h2T = h2Ts[bp]
            for l, i in enumerate(blk):
                xk = ("Xb", bp, l)
                h2 = h2s[l % 2]
                hk = ("h2", l % 2)
                c2 = 2 + (l % 2)
                DMA("sp", Xb[:, l, :], x[i], w=[xk])
                pm = [newps(), newps()]
                for half in range(2):
                    for mc in range(8):
                        MM(PS(pm[half]), ogT[:, mc, i * 128:(i + 1) * 128], wout[:, mc, half * 512:(half + 1) * 512],
                           mc == 0, mc == 7, [("ogTa", i), ("ogTb", i), "wout"], [("ps", pm[half])])
                for half in range(2):
                    TT("dve", Xb[:, l, half * 512:(half + 1) * 512], Xb[:, l, half * 512:(half + 1) * 512], PS(pm[half]), ALU.add,
                       [xk, ("ps", pm[half])], [xk])
                rel(*pm)
                ACT(h2, Xb[:, l, :], AF.Square, [xk], [hk, ("ss2", c2)], accum=sm[:, c2:c2 + 1])
                rstd_from_ss(sm[:, c2:c2 + 1], D, ("ss2", c2))
                STT(h2, Xb[:, l, :], sm[:, c2:c2 + 1], nffn, ALU.mult, ALU.mult, [xk, ("ss2", c2), "nffn"], [hk])
                pt = newps()
                for kc in range(8):
                    TR(PST(pt)[:, kc, :], h2[:, kc * 128:(kc + 1) * 128], [hk], [("ps", pt)])
                CP("act", h2T[:, :, l * 128:(l + 1) * 128], PST(pt), [("ps", pt)], [("h2T", bp, l)])
                rel(pt)

        def up_phase(bi):
            blk = blocks[bi]
            nl = len(blk)
            ntok = nl * 128
            bp = bi % 2
            h2T = h2Ts[bp]
            h2keys = [("h2T", bp, l) for l in range(nl)]
            for cb in range(8):
                wb = wupb[cb % 2]
                wk = ("wupb", cb % 2)
                DMA("sp", wb, wu_s[cb], r=[("wu_s", cb)], w=[wk])
                for f in range(4):
                    ffc = cb * 4 + f
                    pb = newps()
                    for kc in range(8):
                        MM(PS(pb)[:, 0:ntok], wb[:, kc, f * 128:(f + 1) * 128], h2T[:, kc, 0:ntok], kc == 0, kc == 7,
                           h2keys + [wk], [("ps", pb)])
                    rb = rl[ffc % 2]
                    rk = ("rl", ffc % 2)
                    ACT(rb[:, 0:ntok], PS(pb)[:, 0:ntok], AF.Relu, [("ps", pb)], [rk])
                    rel(pb)
                    TT("pool", aT[:, ffc, 0:ntok], rb[:, 0:ntok], rb[:, 0:ntok], ALU.mult, [rk], [("aT", ffc)])

        def down_phase(bi):
            blk = blocks[bi]
            nl = len(blk)
            bp = bi % 2
            Xb = Xbs[bp]
            for half in range(2):
                banks = [newps() for _ in range(nl)]
                for q in range(8):
                    wi = (2 * bi * 8 + half * 8 + q) % 4
                    wb = wdn4[wi]
                    wk = ("wdn4", wi)
                    DMA("sp", wb, wd_s[q][:, :, half * 512:(half + 1) * 512], r=[("wd_s", q)], w=[wk])
                    for f in range(4):
                        ffc = q * 4 + f
                        for l in range(nl):
                            MM(PS(banks[l]), aT[:, ffc, l * 128:(l + 1) * 128], wb[:, f, :],
                               ffc == 0, ffc == 31, [("aT", ffc), wk], [("ps", banks[l])])
                for l in range(nl):
                    xk = ("Xb", bp, l)
                    TT("dve", Xb[:, l, half * 512:(half + 1) * 512], Xb[:, l, half * 512:(half + 1) * 512], PS(banks[l]),
                       ALU.add, [xk, ("ps", banks[l])], [xk])
                    rel(banks[l])
            for l, i in enumerate(blk):
                xk = ("Xb", bp, l)
                c3 = 4 + (l % 2)
                ACT(h2s[l % 2], Xb[:, l, :], AF.Square, [xk], [("h2", l % 2), ("ss3", c3)], accum=sm[:, c3:c3 + 1])
                rstd_from_ss(sm[:, c3:c3 + 1], D, ("ss3", c3))
                STT(Xb[:, l, :], Xb[:, l, :], sm[:, c3:c3 + 1], nfin, ALU.mult, ALU.mult, [xk, ("ss3", c3), "nfin"], [xk])
                ydmas.append(DMA("sp", y[i], Xb[:, l, :], r=[xk]))

        prologue(0)
        for bi in range(len(blocks)):
            up_phase(bi)
            if bi + 1 < len(blocks):
                prologue(bi + 1)
            down_phase(bi)

        S.final_wait("sp", ydmas + out_dmas)
        S.emit()
    return nc


def _consts():
    f32 = np.float32
    half = 64
    inv_freq = (f32(10000.0) ** (-(np.arange(half, dtype=f32)) / f32(half))).astype(f32)
    pos = np.zeros((128, NT), dtype=f32)
    for i in range(16):
        pos[:, i] = i * 128 + np.arange(128)
    pos[:, 16] = 16384 + (np.arange(128) % 8)
    ang = (pos[:, :, None] * inv_freq[None, None, :]).astype(f32)
    cs = np.stack([np.cos(ang).astype(f32), np.sin(ang).astype(f32)], axis=1)
    gam = np.array(GAM, dtype=np.float64)
    lg = np.log(gam)
    gtab = np.zeros((128, 2, 4, 4), dtype=f32)
    p = np.arange(128)
    for kind, j in ((0, p), (1, p % 8)):
        gtab[:, kind, 0, :] = np.exp((j[:, None] + 1) * lg[None, :])
        gtab[:, kind, 1, :] = np.exp(-(j[:, None] + 1) * lg[None, :]) * SCALE
        gtab[:, kind, 3, :] = np.exp(2 * (j[:, None] + 1) * lg[None, :])
    gtab[:, 0, 2, :] = np.exp(128 * lg)[None, :]
    gtab[:, 1, 2, :] = np.exp(8 * lg)[None, :]
    s = p[:, None]
    t = p[None, :]
    masks = np.zeros((128, 3, 128), dtype=f32)
    masks[:, 0, :] = (s <= t)
    masks[:, 1, :] = (s <= t) & (s // 64 == t // 64)
    masks[:, 2, :] = (s <= t) & (s // 8 == t // 8)
    scanm = np.ones((128, 2, 512), dtype=f32)
    scanm[:, 0, ::64] = 0.0
    scanm[:, 1, ::8] = 0.0
    colmask = np.zeros((128, 16, 128), dtype=f32)
    rowmask = np.zeros((128, 16, 128), dtype=f32)
    for j in range(16):
        colmask[:, j, 8 * j:8 * j + 8] = 1.0
        rowmask[8 * j:8 * j + 8, j, :] = 1.0
    bf = ml_dtypes.bfloat16
    return {
        "ident": np.eye(128, dtype=f32).astype(bf),
        "cs": np.ascontiguousarray(cs),
        "gtab": gtab,
        "masks": masks.astype(bf),
        "scanm": scanm,
        "colmask": colmask.astype(bf),
        "rowmask": rowmask.astype(bf),
    }


_NC_CACHE = {}


def kernel(x_prompt, x_sample, state_ret, state_hgrn, norm_mix_w, w_in, ret_norm_w, hgrn_norm_w,
           lb_logits, w_out, norm_ffn_w, w_up, w_down, final_norm_w):
    f32 = np.float32
    asf = lambda a: np.ascontiguousarray(np.asarray(a, dtype=f32))
    x_prompt, x_sample = asf(x_prompt), asf(x_sample)
    state_ret, state_hgrn = asf(state_ret), asf(state_hgrn)
    if "nc" not in _NC_CACHE:
        for w_ in (60, 30, 20, 10, 4, 0):
            try:
                _NC_CACHE["nc"] = build_nc(window=w_)
                break
            except RuntimeError as ex_:
                if "scheduler deadlock" not in str(ex_):
                    raise
    nc = _NC_CACHE["nc"]
    cst = _consts()
    bc = lambda v, n: np.ascontiguousarray(np.broadcast_to(asf(v).reshape(1, -1), (128, n)))
    shared = {
        "w_in": asf(w_in)[0], "w_out": asf(w_out)[0], "w_up": asf(w_up)[0], "w_down": asf(w_down)[0],
        "nmix": bc(norm_mix_w[0], D), "nffn": bc(norm_ffn_w[0], D), "nfin": bc(final_norm_w, D),
        "rnw": bc(np.tile(asf(ret_norm_w)[0], 4), 512), "hnw": bc(np.tile(asf(hgrn_norm_w)[0], 4), 512),
        "lbl": np.ascontiguousarray(asf(lb_logits).reshape(2, 4, 128).transpose(2, 0, 1)),
    }
    shared.update(cst)
    in_maps = []
    for c in range(NCORES):
        xc = np.concatenate([x_prompt[c], x_sample[16 * c:16 * c + 16].reshape(128, D)], axis=0).reshape(NT, 128, D)
        m = dict(shared)
        m["x"] = np.ascontiguousarray(xc)
        m["sret"] = np.ascontiguousarray(state_ret[0, 16 * c:16 * c + 16])
        m["shg"] = np.ascontiguousarray(state_hgrn[0, 16 * c:16 * c + 16])
        in_maps.append(m)
    res = run_bass_kernel_spmd(nc, in_maps, core_ids=list(range(NCORES)))
    rs = res.results
    yall = np.stack([r["y"].reshape(NT * 128, D) for r in rs], axis=0)
    y_prompt = np.ascontiguousarray(yall[:, :2048, :])
    y_sample = np.ascontiguousarray(yall[:, 2048:, :].reshape(128, 8, D))
    ret_p = np.stack([r["srp"] for r in rs], axis=0)[None]
    hg_p = np.stack([r["shp"] for r in rs], axis=0)[None]
    ret_s = np.concatenate([r["srs"] for r in rs], axis=0)[None]
    hg_s = np.concatenate([r["shs"] for r in rs], axis=0)[None]
    return (y_prompt.astype(f32), y_sample.astype(f32), ret_p.astype(f32), hg_p.astype(f32),
            ret_s.astype(f32), hg_s.astype(f32))
```

```python
import numpy as np
import ml_dtypes
from contextlib import ExitStack

import concourse.bass as bass
import concourse.mybir as mybir
from concourse.bass_utils import run_bass_kernel_spmd

F32 = mybir.dt.float32
BF16 = mybir.dt.bfloat16
U8 = mybir.dt.uint8
AF = mybir.ActivationFunctionType
ALU = mybir.AluOpType

NCORES = 8
NT = 17
D = 1024
DFF = 4096
EPS = 1e-6
SCALE = 128.0 ** -0.5
GAM = [1.0 - 2.0 ** (-5.0 - h) for h in range(4)]


class _Op:
    __slots__ = ("id", "eng", "fn", "deps", "is_dma", "slot", "slot_count", "needs_inc", "sem_idx", "sem_val",
                 "cost", "nbytes", "seg", "fin", "aset")


class Sched:
    ENGS = ("pe", "act", "dve", "pool", "sp")
    DMAQ = ("sp", "pool")

    def __init__(self, nc, n_dma_sems=16, epoch=3000):
        self.nc = nc
        self.ops = []
        self.last_writer = {}
        self.readers = {}
        self.part_writer = {}
        self.part_readers = {}
        self.n_dma_sems = n_dma_sems
        self.epoch = epoch
        self.seg = 0
        self.op_pbs = {}
        self.first_of = {}
        self.pbs = []
        self.window = 10
        self.filler_fn = None
        self.filler_deps = []
        import os
        self.use_blevel = os.environ.get("SCHED_BLEVEL", "1") == "1"

    def _new(self, eng, fn, is_dma, cost, nbytes=0):
        op = _Op()
        op.id = len(self.ops)
        op.eng = eng
        op.fn = fn
        op.is_dma = is_dma
        op.needs_inc = False
        op.slot = None
        op.slot_count = 0
        op.sem_idx = 0
        op.sem_val = 0
        op.deps = []
        op.cost = cost
        op.nbytes = nbytes
        op.seg = self.seg
        op.fin = 0.0
        op.aset = None
        return op

    def _add(self, eng, fn, reads, writes, is_dma, cost, nbytes=0):
        op = self._new(eng, fn, is_dma, cost, nbytes)
        deps = set()

        def is_part(k_):
            return isinstance(k_, tuple) and len(k_) == 3 and k_[0] == "PART"

        for r in reads:
            if is_part(r):
                base, part = r[1], r[2]
                w = self.last_writer.get(base)
                if w is not None:
                    deps.add(w)
                w = self.part_writer.get(base, {}).get(part)
                if w is not None:
                    deps.add(w)
            else:
                w = self.last_writer.get(r)
                if w is not None:
                    deps.add(w)
                for w in self.part_writer.get(r, {}).values():
                    deps.add(w)
        for w_ in writes:
            if is_part(w_):
                base, part = w_[1], w_[2]
                w = self.last_writer.get(base)
                if w is not None:
                    deps.add(w)
                w = self.part_writer.get(base, {}).get(part)
                if w is not None:
                    deps.add(w)
                for rd in self.readers.get(base, ()):
                    deps.add(rd)
                for rd in self.part_readers.get(base, {}).get(part, ()):
                    deps.add(rd)
            else:
                w = self.last_writer.get(w_)
                if w is not None:
                    deps.add(w)
                for w in self.part_writer.get(w_, {}).values():
                    deps.add(w)
                for rd in self.readers.get(w_, ()):
                    deps.add(rd)
                for lst in self.part_readers.get(w_, {}).values():
                    for rd in lst:
                        deps.add(rd)
        deps.discard(op.id)
        op.deps = sorted(deps)
        for r in reads:
            if is_part(r):
                self.part_readers.setdefault(r[1], {}).setdefault(r[2], []).append(op.id)
            else:
                self.readers.setdefault(r, []).append(op.id)
        for w_ in writes:
            if is_part(w_):
                self.part_writer.setdefault(w_[1], {})[w_[2]] = op.id
                self.part_readers.setdefault(w_[1], {})[w_[2]] = []
            else:
                self.last_writer[w_] = op.id
                self.readers[w_] = []
                self.part_writer[w_] = {}
                self.part_readers[w_] = {}
        self.ops.append(op)
        for kk in list(reads) + list(writes):
            if isinstance(kk, tuple) and len(kk) == 2 and kk[0] == "ps" and isinstance(kk[1], PB):
                pb = kk[1]
                if op.id not in pb.ops:
                    pb.ops.append(op.id)
                    self.op_pbs.setdefault(op.id, []).append(pb)
                if pb.first is None:
                    pb.first = op.id
                    pb.seg = self.seg
                    self.first_of[op.id] = pb
        return op.id

    def compute(self, eng, fn, reads=(), writes=(), cost=0.3, aset=None):
        oid = self._add(eng, fn, tuple(reads), tuple(writes), False, cost)
        self.ops[oid].aset = aset
        return oid

    def dma(self, queue, fn, reads=(), writes=(), nbytes=0):
        return self._add(queue, fn, tuple(reads), tuple(writes), True, 0.1 if queue == "sp" else 1.0, nbytes)

    def final_wait(self, eng, dep_ids):
        op = self._new(eng, None, False, 0.01)
        op.deps = sorted(set(dep_ids))
        self.ops.append(op)
        return op.id

    def barrier(self):
        self.seg += 1

    def schedule(self):
        import heapq
        ops = self.ops
        nseg = self.seg + 1
        order = {e: [] for e in self.ENGS}
        t_base = 0.0
        slot_last = {q: {} for q in self.DMAQ}
        slot_counts = {q: {} for q in self.DMAQ}
        dma_n = {q: 0 for q in self.DMAQ}
        succ = [[] for _ in ops]
        for op in ops:
            for d in op.deps:
                succ[d].append(op.id)
        self.seg_last = []
        for sg in range(nseg):
            seg_ops = [op for op in ops if op.seg == sg]
            if not seg_ops:
                self.seg_last.append([])
                continue
            indeg = {}
            ready_t = {}
            pend = {e: [] for e in self.ENGS}
            avail = {e: [] for e in self.ENGS}
            for op in seg_ops:
                n = 0
                rt = t_base
                for d in op.deps:
                    if ops[d].seg == sg:
                        n += 1
                indeg[op.id] = n
                ready_t[op.id] = rt
                if n == 0:
                    heapq.heappush(pend[op.eng], (rt, op.id))
            bl = {}
            for op in reversed(seg_ops):
                m = 0.0
                for s_ in succ[op.id]:
                    if ops[s_].seg == sg and s_ in bl:
                        m = max(m, bl[s_] + 0.4)
                xc = op.cost + (op.nbytes / 200e3 + 2.0 if op.is_dma else 0.0)
                bl[op.id] = m + xc
            PRI = self.use_blevel
            free = {e: t_base for e in self.ENGS}
            cur_set = [None]
            HOP = 0.2
            dma_pipe = t_base
            remaining = len(seg_ops)
            seg_end = t_base
            seg_order = {e: [] for e in self.ENGS}
            seg_pbs = sorted([pb for pb in self.pbs if pb.seg == sg and pb.first is not None], key=lambda p_: p_.first)
            for r_, pb in enumerate(seg_pbs):
                pb.idx = r_
            granted = set()
            fr_pos = [0]
            bank_occ = [None] * 8

            def frontier_idx():
                while fr_pos[0] < len(seg_pbs) and seg_pbs[fr_pos[0]].idx in granted:
                    fr_pos[0] += 1
                return seg_pbs[fr_pos[0]].idx if fr_pos[0] < len(seg_pbs) else 1 << 60

            def bank_for(pb, t):
                fidx = frontier_idx()
                if pb.idx > fidx + self.window:
                    return None
                nfree = 0
                bestb = None
                for b_ in range(8):
                    oc = bank_occ[b_]
                    if oc is None:
                        ft = t_base
                    elif oc.n_sched == len(oc.ops):
                        ft = oc.max_fin + 0.2
                    else:
                        continue
                    nfree += 1
                    if bestb is None or ft < bestb[1]:
                        bestb = (b_, ft)
                if bestb is None:
                    return None
                if pb.idx != fidx and nfree < 3:
                    return None
                return bestb

            while remaining:
                best = None
                for e in self.ENGS:
                    t = free[e]
                    while pend[e] and pend[e][0][0] <= t:
                        rt, oid = heapq.heappop(pend[e])
                        avail[e].append(oid)
                    keyf = (lambda o_: (-bl[o_], o_)) if PRI else (lambda o_: o_)
                    cands = []
                    if e == "pe":
                        pool_ = list(avail[e]) + [o_ for (_rt, o_) in pend[e]]
                        for o_ in pool_:
                            st_ = max(t, ready_t[o_])
                            pb_ = self.first_of.get(o_)
                            bsel = None
                            if pb_ is not None:
                                r_ = bank_for(pb_, st_)
                                if r_ is None:
                                    continue
                                bsel = r_[0]
                                st_ = max(st_, r_[1])
                            cands.append((st_, keyf(o_), o_, bsel))
                        if not cands:
                            continue
                        st_, _k, pick, bsel = min(cands, key=lambda c_: (c_[0], c_[1]))
                        cand = (st_, pick, e, pick in avail[e], bsel)
                    elif avail[e]:
                        pick = min(avail[e], key=keyf)
                        if e == "act" and cur_set[0] is not None:
                            same = [o_ for o_ in avail[e] if ops[o_].aset in (None, cur_set[0])]
                            if same:
                                pick = min(same, key=keyf)
                        cand = (t, pick, e, True, None)
                    elif pend[e]:
                        cand = (pend[e][0][0], pend[e][0][1], e, False, None)
                    else:
                        continue
                    if best is None or cand[:2] < best[:2]:
                        best = cand
                if best is None:
                    raise RuntimeError("scheduler deadlock")
                start, oid, e, from_avail, bsel = best
                if from_avail:
                    avail[e].remove(oid)
                else:
                    pend[e] = [x_ for x_ in pend[e] if x_[1] != oid]
                    heapq.heapify(pend[e])
                if bsel is not None:
                    pb_ = self.first_of[oid]
                    prev_occ = bank_occ[bsel]
                    if prev_occ is not None:
                        ops[oid].deps = sorted(set(ops[oid].deps) | set(prev_occ.ops))
                    bank_occ[bsel] = pb_
                    pb_.bank = bsel
                    granted.add(pb_.idx)
                op = ops[oid]
                if e == "pe" and self.filler_fn is not None and start - free[e] > 2.6 and free[e] > t_base:
                    tf = free[e] + 2.0
                    while tf < start - 0.4:
                        for _r in range(3):
                            fop = self._new("pe", self.filler_fn, False, 0.06)
                            fop.seg = sg
                            fop.deps = list(self.filler_deps)
                            fop.fin = tf + 0.06
                            self.ops.append(fop)
                            seg_order["pe"].append(fop.id)
                            tf += 0.06
                        tf += 2.0
                    ops = self.ops
                if e == "act" and op.aset is not None and op.aset != cur_set[0]:
                    start += 1.3
                    cur_set[0] = op.aset
                if op.is_dma:
                    q = op.eng
                    nsl = self.n_dma_sems if q == "sp" else 4
                    slot = dma_n[q] % nsl
                    dma_n[q] += 1
                    prev = slot_last[q].get(slot)
                    if prev is not None:
                        start = max(start, ops[prev].fin)
                        op.deps = sorted(set(op.deps) | {prev})
                    slot_last[q][slot] = oid
                    slot_counts[q][slot] = slot_counts[q].get(slot, 0) + 1
                    op.slot = slot
                    op.slot_count = slot_counts[q][slot]
                    issue_end = start + op.cost
                    xfer = op.nbytes / 200e3
                    begin = max(issue_end, dma_pipe)
                    dma_pipe = begin + xfer
                    op.fin = begin + xfer + 2.0
                    free[e] = issue_end
                else:
                    op.fin = start + op.cost
                    free[e] = op.fin
                seg_end = max(seg_end, op.fin)
                seg_order[e].append(oid)
                for pb_ in self.op_pbs.get(oid, ()):
                    pb_.n_sched += 1
                    pb_.max_fin = max(pb_.max_fin, op.fin)
                remaining -= 1
                for s_ in succ[oid]:
                    if ops[s_].seg != sg:
                        continue
                    ready_t[s_] = max(ready_t[s_], op.fin + (HOP if ops[s_].eng != op.eng else 0.05))
                    indeg[s_] -= 1
                    if indeg[s_] == 0:
                        heapq.heappush(pend[ops[s_].eng], (ready_t[s_], s_))
            lasts = []
            for e in self.ENGS:
                for oid in reversed(seg_order[e]):
                    if ops[oid].fn is not None and not ops[oid].is_dma:
                        lasts.append(oid)
                        break
                lasts.extend(oid for oid in seg_order[e] if ops[oid].is_dma)
            self.seg_last.append(lasts)
            for e in self.ENGS:
                if sg > 0:
                    bop = self._new(e, None, False, 0.01)
                    bop.seg = sg
                    bop.deps = sorted(set(self.seg_last[sg - 1]))
                    self.ops.append(bop)
                    order[e].append(bop.id)
                order[e].extend(seg_order[e])
            t_base = seg_end
            print(f"[sched] segment {sg}: {len(seg_ops)} ops, est end {seg_end:.1f} us")
        self.eng_order = order

    def emit(self):
        self.schedule()
        nc = self.nc
        ops = self.ops
        eng_ops = {e: [ops[i] for i in self.eng_order[e]] for e in self.ENGS}
        for e in self.ENGS:
            for op in eng_ops[e]:
                for d in op.deps:
                    dop = ops[d]
                    if dop.is_dma:
                        continue
                    if dop.eng == op.eng and op.eng == "pe":
                        continue
                    dop.needs_inc = True
        n_epochs = {}
        for e in self.ENGS:
            cnt = 0
            for op in eng_ops[e]:
                if op.is_dma or not op.needs_inc:
                    continue
                op.sem_idx = cnt // self.epoch
                op.sem_val = cnt % self.epoch + 1
                cnt += 1
            n_epochs[e] = max(1, (cnt + self.epoch - 1) // self.epoch)
        with ExitStack() as st:
            eng_sems = {e: [st.enter_context(nc.semaphore(f"s_{e}_{i}")) for i in range(n_epochs[e])]
                        for e in self.ENGS}
            dma_sems = {q: [st.enter_context(nc.semaphore(f"s_dma_{q}_{i}")) for i in range(self.n_dma_sems)]
                        for q in self.DMAQ}
            block = st.enter_context(nc.Block())

            def run_engine(ename, eng):
                waited = {}
                for op in eng_ops[ename]:
                    for d in op.deps:
                        dop = ops[d]
                        if dop.is_dma:
                            key = ("d", dop.eng, dop.slot)
                            sem = dma_sems[dop.eng][dop.slot]
                            val = 16 * dop.slot_count
                        else:
                            if dop.fn is None:
                                continue
                            if dop.eng == ename and ename == "pe":
                                continue
                            key = (dop.eng, dop.sem_idx)
                            sem = eng_sems[dop.eng][dop.sem_idx]
                            val = dop.sem_val
                        if waited.get(key, 0) >= val:
                            continue
                        waited[key] = val
                        eng.wait_ge(sem, val)
                    if op.fn is None:
                        continue
                    ins = op.fn(eng)
                    if op.is_dma:
                        ins.then_inc(dma_sems[op.eng][op.slot], 16)
                    elif op.needs_inc:
                        ins.then_inc(eng_sems[ename][op.sem_idx], 1)

            @block.tensor
            def _(eng):
                run_engine("pe", eng)

            @block.scalar
            def _(eng):
                run_engine("act", eng)

            @block.vector
            def _(eng):
                run_engine("dve", eng)

            @block.gpsimd
            def _(eng):
                run_engine("pool", eng)

            @block.sync
            def _(eng):
                run_engine("sp", eng)


ARENA_LOG = []


class PB:
    __slots__ = ("idx", "bank", "ops", "first", "n_sched", "max_fin", "seg")

    def __init__(self, idx):
        self.idx = idx
        self.bank = None
        self.ops = []
        self.first = None
        self.n_sched = 0
        self.max_fin = 0.0
        self.seg = 0


class LazyAP:
    def __init__(self, pb, base, chain=()):
        self.pb = pb
        self.base = base
        self.chain = chain

    def _ext(self, item):
        return LazyAP(self.pb, self.base, self.chain + (item,))

    def __getitem__(self, key):
        return self._ext(("getitem", key))

    def rearrange(self, pat, **kw):
        return self._ext(("rearrange", pat, kw))

    def bitcast(self, dt):
        return self._ext(("bitcast", dt))

    def resolve(self):
        ap = self.base[self.pb.bank if self.pb.bank is not None else 0]
        for it in self.chain:
            if it[0] == "getitem":
                ap = ap[it[1]]
            elif it[0] == "rearrange":
                ap = ap.rearrange(it[1], **it[2])
            else:
                ap = ap.bitcast(it[1])
        return ap

    @property
    def shape(self):
        return self.resolve().shape

    @property
    def dtype(self):
        return self.resolve().dtype


def RZ(x):
    return x.resolve() if isinstance(x, LazyAP) else x


class Arena:
    def __init__(self, ap_u8, nbytes):
        self.ap = ap_u8
        self.n = nbytes
        self.off = 0
        self.peak = 0

    def alloc(self, free_shape, dtype):
        esz = 4 if dtype == F32 else 2
        n = 1
        for s in free_shape:
            n *= s
        nb = (n * esz + 63) // 64 * 64
        assert self.off + nb <= self.n, f"arena overflow: need {self.off + nb} > {self.n}"
        v = self.ap[:, self.off:self.off + n * esz].bitcast(dtype)
        import inspect
        fr = inspect.stack()[1]
        ARENA_LOG.append((self.off, n * esz, "f32" if dtype == F32 else "bf16", fr.lineno, (fr.code_context or [""])[0].strip()[:60]))
        self.off += nb
        self.peak = max(self.peak, self.off)
        if len(free_shape) > 1:
            names = [f"a{i}" for i in range(len(free_shape))]
            v = v.rearrange("p (" + " ".join(names) + ") -> p " + " ".join(names),
                            **{nm: s for nm, s in zip(names, free_shape)})
        return v


def build_nc(debug=False, window=60):
    nc = bass.Bass("TRN2", target_bir_lowering=False)

    def din(name, shape, dt=F32):
        return nc.dram_tensor(name, list(shape), dt, kind="ExternalInput").ap()

    def dout(name, shape, dt=F32):
        return nc.dram_tensor(name, list(shape), dt, kind="ExternalOutput").ap()

    x = din("x", [NT, 128, D])
    sret = din("sret", [16, 4, 128, 128])
    shg = din("shg", [16, 4, 128, 128])
    w_in = din("w_in", [D, 4096])
    w_out = din("w_out", [D, D])
    w_up = din("w_up", [D, DFF])
    w_down = din("w_down", [DFF, D])
    nmix_d = din("nmix", [128, D])
    nffn_d = din("nffn", [128, D])
    nfin_d = din("nfin", [128, D])
    rnw_d = din("rnw", [128, 512])
    hnw_d = din("hnw", [128, 512])
    lbl_d = din("lbl", [128, 2, 4])
    ident_d = din("ident", [128, 128], BF16)
    cs_d = din("cs", [128, 2, NT, 64])
    gtab_d = din("gtab", [128, 2, 4, 4])
    masks_d = din("masks", [128, 3, 128], BF16)
    scanm_d = din("scanm", [128, 2, 512])
    colmask_d = din("colmask", [128, 16, 128], BF16)
    rowmask_d = din("rowmask", [128, 16, 128], BF16)

    y = dout("y", [NT, 128, D])
    srp = dout("srp", [4, 128, 128])
    shp = dout("shp", [4, 128, 128])
    srs = dout("srs", [16, 4, 128, 128])
    shs = dout("shs", [16, 4, 128, 128])

    wu_s = nc.dram_tensor("wu_s", [8, 128, 8, 512], BF16, kind="Internal").ap()
    wd_s = nc.dram_tensor("wd_s", [8, 128, 4, 1024], BF16, kind="Internal").ap()
    wo_s = nc.dram_tensor("wo_s", [128, 8, 1024], BF16, kind="Internal").ap()

    S = Sched(nc)
    S.window = window
    NB = 207 * 1024
    with ExitStack() as ctx:
        arena_t = ctx.enter_context(nc.sbuf_tensor("arena", [128, NB], U8))
        ps = [ctx.enter_context(nc.psum_tensor(f"ps{k}", [128, 512], F32)) for k in range(8)]
        A = Arena(arena_t[:, :], NB)

        PSB = [ps[i_][:, :] for i_ in range(8)]

        def PS(k):
            return LazyAP(k, PSB)

        def PS4(k):
            return LazyAP(k, PSB).rearrange("p (h d) -> p h d", h=4)

        def PST(k):
            return LazyAP(k, PSB).bitcast(BF16).rearrange("p (k t) -> p k t", k=8)

        def fsz(ap):
            n = 1
            for d_ in ap.shape[1:]:
                n *= d_
            return n

        def DMA(q, out, in_, r=(), w=()):
            esz = 4 if out.dtype == F32 else 2
            nb = out.shape[0] * fsz(out) * esz
            return S.dma(q, lambda e: e.dma_start(out=out, in_=in_), r, w, nbytes=nb)

        def MM(out, lhsT, rhs, start, stop, r, w):
            S.compute("pe", lambda e: e.matmul(RZ(out), lhsT=lhsT, rhs=rhs, start=start, stop=stop), r, w,
                      cost=0.036 + fsz(rhs) * 0.00039)

        def TR(out, in_, r, w):
            S.compute("pe", lambda e: e.transpose(out=RZ(out), in_=in_, identity=ident), list(r) + ["ident"], w, cost=0.085)

        def ACT(out, in_, func, r, w, scale=1.0, bias=None, accum=None):
            def f(e):
                kw = {}
                if bias is not None:
                    kw["bias"] = bias
                if accum is not None:
                    kw["accum_out"] = accum
                return e.activation(out=RZ(out), in_=RZ(in_), func=func, scale=scale, **kw)
            aset = {AF.Silu: 18, AF.Tanh: 18, AF.Sigmoid: 2, AF.Ln: 6, AF.Exp: 6, AF.Sqrt: 3}.get(func)
            S.compute("act", f, r, w, cost=0.12 + fsz(in_) / 1200.0 + (0.1 if accum is not None else 0.0), aset=aset)

        def ecost(eng, ap):
            return (0.1 + fsz(ap) / 960.0) if eng == "dve" else (0.15 + fsz(ap) / 450.0)

        def TT(eng, out, in0, in1, op, r, w):
            S.compute(eng, lambda e: e.tensor_tensor(out=RZ(out), in0=RZ(in0), in1=RZ(in1), op=op), r, w, cost=ecost(eng, out))

        def TS(eng, out, in0, s1, s2, op0, op1, r, w):
            if s2 is None:
                S.compute(eng, lambda e: e.tensor_scalar(out=RZ(out), in0=RZ(in0), scalar1=s1, scalar2=None, op0=op0), r, w,
                          cost=ecost(eng, out))
            else:
                S.compute(eng, lambda e: e.tensor_scalar(out=RZ(out), in0=RZ(in0), scalar1=s1, scalar2=s2, op0=op0, op1=op1), r, w,
                          cost=ecost(eng, out))

        def STT(out, in0, scalar, in1, op0, op1, r, w):
            S.compute("dve", lambda e: e.scalar_tensor_tensor(out=RZ(out), in0=RZ(in0), scalar=scalar, in1=RZ(in1), op0=op0, op1=op1), r, w,
                      cost=ecost("dve", out))

        def CP(eng, out, in_, r, w):
            if eng == "act":
                ACT(out, in_, AF.Copy, r, w)
            else:
                S.compute(eng, lambda e: e.tensor_copy(out=RZ(out), in_=RZ(in_)), r, w, cost=ecost(eng, out))

        def RECIP(out, in_, r, w):
            S.compute("dve", lambda e: e.reciprocal(out=out, in_=in_), r, w, cost=ecost("dve", out))

        def MEMSET(eng, ap, val, w):
            S.compute(eng, lambda e: e.memset(ap, val), (), w, cost=ecost(eng, ap))

        def bc_mid(ap2, n):
            return ap2.unsqueeze(1).broadcast_to([128, n, ap2.shape[1]])

        def bc_last(ap2, n):
            return ap2.unsqueeze(2).broadcast_to([128, ap2.shape[1], n])

        held = set()
        alloc_n = [0]

        def newps():
            pb = PB(alloc_n[0])
            alloc_n[0] += 1
            assert len(held) < 8, "more than 8 live PSUM banks"
            held.add(pb)
            S.pbs.append(pb)
            return pb

        def rel(*ks):
            for k in ks:
                held.discard(k)

        ogT = A.alloc([8, NT * 128], BF16)
        ident = A.alloc([128], BF16)
        cs = A.alloc([2, NT, 64], F32)
        gtab = A.alloc([2, 4, 4], F32)
        masks = A.alloc([3, 128], BF16)
        scanm = A.alloc([2, 512], F32)
        rnw = A.alloc([512], F32)
        hnw = A.alloc([512], F32)
        lbt = A.alloc([2, 4], F32)
        lb = A.alloc([4], F32)
        oml = A.alloc([4], F32)
        homl = A.alloc([4], F32)
        lbh = A.alloc([4], F32)
        sm = A.alloc([64], F32)
        junk = A.alloc([128], BF16)
        epst = A.alloc([1], F32)
        win = A.alloc([8, 4096], BF16)
        mark_win = A.off - 8 * 4096 * 2
        mark = A.off

        ident_dma = DMA("sp", ident, ident_d, w=["ident"])
        import os as _os
        if _os.environ.get("NO_FILLERS", "0") != "1":
            S.filler_fn = lambda e: e.ldweights(ident)
            S.filler_deps = [ident_dma]
        DMA("sp", cs, cs_d, w=["cs"])
        DMA("sp", gtab, gtab_d, w=["gtab"])
        DMA("sp", masks, masks_d, w=["masks"])
        DMA("sp", scanm, scanm_d, w=["scanm"])
        DMA("sp", rnw, rnw_d, w=["rnw"])
        DMA("sp", hnw, hnw_d, w=["hnw"])
        DMA("sp", lbt, lbl_d, w=["lbt"])
        MEMSET("pool", epst, EPS, ["epst"])
        TT("dve", lb, lbt[:, 0, :], lbt[:, 1, :], ALU.subtract, ["lbt"], ["lb"])
        ACT(lb, lb, AF.Sigmoid, ["lb"], ["lb"])
        TS("dve", oml, lb, -1.0, 1.0, ALU.mult, ALU.add, ["lb"], ["oml"])
        TS("dve", homl, oml, 0.5, None, ALU.mult, None, ["oml"], ["homl"])
        TT("dve", lbh, lb, homl, ALU.add, ["lb", "homl"], ["lbh"])

        nmix = A.alloc([D], F32)
        Sr = A.alloc([4, 128], F32)
        Sbr = A.alloc([4, 128], BF16)
        Sh = A.alloc([4, 128], F32)
        Sbh = [A.alloc([4, 128], BF16) for _ in range(2)]
        mark_mix = A.off

        def alloc_tile_bufs():
            b = {}
            b["xt"] = A.alloc([D], F32)
            b["hbf"] = A.alloc([D], BF16)
            b["hT"] = A.alloc([8, 128], BF16)
            b["rt"] = [A.alloc([4, 64], F32) for _ in range(4)]
            b["qrot"] = A.alloc([4, 128], BF16)
            b["ktm"] = A.alloc([4, 128], BF16)
            b["vp"] = A.alloc([4, 128], BF16)
            b["gate"] = A.alloc([512], F32)
            b["qkT"] = A.alloc([8, 128], BF16)
            b["ATm"] = A.alloc([4, 128], BF16)
            b["og"] = A.alloc([4, 128], BF16)
            b["B"] = [A.alloc([512], F32) for _ in range(4)]
            b["qtT"] = A.alloc([4, 128], BF16)
            b["ktT"] = A.alloc([4, 128], BF16)
            b["ktmh"] = A.alloc([4, 128], BF16)
            b["vph"] = A.alloc([4, 128], BF16)
            b["gateh"] = A.alloc([512], F32)
            b["ogh"] = A.alloc([4, 128], BF16)
            b["ATmh"] = A.alloc([4, 128], BF16)
            return b

        TB = [alloc_tile_bufs() for _ in range(2)]
        NHT = 4
        HTX = [TB[0]["hT"], TB[1]["hT"]] + [A.alloc([8, 128], BF16) for _ in range(NHT - 2)]
        print("mixer arena bytes/partition:", A.off)

        w_in_v = w_in.rearrange("(kc p) n -> p kc n", p=128)
        prevk = []
        for (c0, n_) in ((0, 2), (2, 2), (6, 2), (4, 2)):
            DMA("pool", win[:, :, c0 * 512:(c0 + n_) * 512], w_in_v[:, :, c0 * 512:(c0 + n_) * 512], r=list(prevk),
                w=[("win", cb) for cb in range(c0, c0 + n_)])
            prevk = [("win", c0)]
        DMA("sp", nmix, nmix_d, w=["nmix"])
        w_up_v = w_up.rearrange("(kc p) n -> p kc n", p=128)
        for cb in range(8):
            DMA("pool", wu_s[cb], w_up_v[:, :, cb * 512:(cb + 1) * 512], r=[("win", 0), ("win", 4)], w=[("wu_s", cb)])
        DMA("pool", wo_s, w_out.rearrange("(kc p) n -> p kc n", p=128), r=[("win", 0), ("win", 4)], w=["wo_s"])
        w_dn_v = w_down.rearrange("(q f p) n -> q p f n", f=4, p=128)
        for q in range(8):
            DMA("pool", wd_s[q], w_dn_v[q], r=[("win", 0), ("win", 4)], w=[("wd_s", q)])

        MEMSET("pool", Sr, 0.0, ["Sr"])
        MEMSET("pool", Sh, 0.0, ["Sh"])
        MEMSET("pool", Sbr, 0.0, ["Sbr"])
        MEMSET("pool", Sbh[1], 0.0, [("Sbh", 1)])

        def rstd_from_ss(ss_ap, n, key):
            ACT(ss_ap, ss_ap, AF.Ln, [key, "epst"], [key], scale=1.0 / n, bias=epst[:, 0:1])
            ACT(ss_ap, ss_ap, AF.Exp, [key], [key], scale=-0.5)

        out_dmas = []
        sample_bufs = {}

        def mixer_tile(i, hchunk):
            kind = 0 if i < 16 else 1
            p = i % 2
            tb = TB[p]
            K = lambda nm: (nm, p)
            xb = tb["xt"]
            hbf, hT = tb["hbf"], HTX[i % NHT]
            rt, qrot, ktm, vp, gate = tb["rt"], tb["qrot"], tb["ktm"], tb["vp"], tb["gate"]
            qkT, ATm, og = tb["qkT"], tb["ATm"], tb["og"]
            B1, B2, B3, B4 = tb["B"]
            qtT, ktT, ktmh, vph, gateh, ogh = tb["qtT"], tb["ktT"], tb["ktmh"], tb["vph"], tb["gateh"], tb["ogh"]
            ss1 = sm[:, p:p + 1]
            DMA("sp", xb, x[i], w=[K("xt")])
            ACT(hbf, xb, AF.Square, [K("xt")], [K("hbf"), K("ss1")], accum=ss1)
            rstd_from_ss(ss1, D, K("ss1"))
            STT(hbf, xb, ss1, nmix, ALU.mult, ALU.mult, [K("xt"), K("ss1"), "nmix"], [K("hbf")])
            pT = newps()
            for kc in range(8):
                TR(PST(pT)[:, kc, :], hbf[:, kc * 128:(kc + 1) * 128], [K("hbf")], [("ps", pT)])
            CP("act", hT, PST(pT), [("ps", pT)], [("hT", i % NHT)])
            rel(pT)

            pr = {}
            for cb in (0, 1, 2, 3):
                pb = newps()
                pr[cb] = pb
                for kc in range(8):
                    MM(PS(pb), hT[:, kc, :], win[:, kc, cb * 512:(cb + 1) * 512], kc == 0, kc == 7,
                       [("hT", i % NHT), ("win", cb)], [("ps", pb)])
                if cb == 1:
                    pass
            cosb = bc_mid(cs[:, 0, i, :], 4)
            sinb = bc_mid(cs[:, 1, i, :], 4)
            for (bank, dst, dkey, sc) in ((pr[0], qrot, K("qrot"), None), (pr[1], ktm, K("ktm"), None)):
                pv = PS(bank).rearrange("p (h two d) -> p h two d", h=4, two=2)
                x1 = pv[:, :, 0, :]
                x2 = pv[:, :, 1, :]
                dv = dst.rearrange("p h (two d) -> p h two d", two=2)
                pk = ("ps", bank)

                def prod(o, a_, b_, okey):
                    if sc is None:
                        TT("dve", o, a_, b_, ALU.mult, [pk, "cs"], [okey])
                    else:
                        STT(o, a_, sc, b_, ALU.mult, ALU.mult, [pk, "cs"], [okey])
                prod(rt[0], x1, cosb, K("rt0"))
                prod(rt[1], x2, sinb, K("rt1"))
                TT("pool", dv[:, :, 0, :], rt[0], rt[1], ALU.subtract, [K("rt0"), K("rt1")], [dkey])
                prod(rt[2], x1, sinb, K("rt2"))
                prod(rt[3], x2, cosb, K("rt3"))
                TT("pool", dv[:, :, 1, :], rt[2], rt[3], ALU.add, [K("rt2"), K("rt3")], [dkey])
            rel(pr[0], pr[1])
            TT("dve", vp, PS4(pr[2]), bc_last(gtab[:, kind, 1, :], 128), ALU.mult, [("ps", pr[2]), "gtab"], [K("vp")])
            ACT(gate, PS(pr[3]), AF.Silu, [("ps", pr[3])], [K("gate")])
            rel(pr[2], pr[3])
            TT("pool", gate, gate, rnw, ALU.mult, [K("gate"), "rnw"], [K("gate")])
            pq = newps()
            for h in range(4):
                TR(PST(pq)[:, h, :], qrot[:, h, :], [K("qrot")], [("ps", pq)])
            for h in range(4):
                TR(PST(pq)[:, 4 + h, :], ktm[:, h, :], [K("ktm")], [("ps", pq)])
            CP("dve", qkT, PST(pq), [("ps", pq)], [K("qkT")])
            rel(pq)
            pa = newps()
            for h in range(4):
                MM(PS4(pa)[:, h, :], qkT[:, 4 + h, :], qkT[:, h, :], True, True, [K("qkT")], [("ps", pa)])
            mk = 0 if kind == 0 else 2
            TT("dve", ATm, PS4(pa), bc_mid(masks[:, mk, :], 4), ALU.mult, [("ps", pa), "masks"], [K("ATm")])
            rel(pa)
            B1h = B1.rearrange("p (h t) -> p h t", h=4)
            B2h = B2.rearrange("p (h t) -> p h t", h=4)
            B3h = B3.rearrange("p (h t) -> p h t", h=4)
            ATh = tb["ATmh"]

            def sample_state(grp, h, src_q, src_v, kt_ap, state_in, state_out, evec, qkey, vkey, ktkey, po_):
                hh = grp * 4 + h
                sb = hh % 4
                sb2 = hh % 2
                S0f, S0b_ = sample_bufs["S0f"], sample_bufs["S0b"]
                Qexp, Vexp = sample_bufs["Qexp"], sample_bufs["Vexp"][sb2]
                vxk = ("Vexp", sb2)
                rowmask = sample_bufs["rowmask"]
                sfk = ("S0f", sb)
                sbk = ("S0b", sb2)
                S0b = {sb: S0b_[sb2]}
                DMA("sp", S0f[sb], state_in[:, h, :, :].rearrange("j p v -> p j v"), r=["winalias"], w=[sfk])
                CP("act", S0b[sb], S0f[sb], [sfk, "winalias"], [sbk])
                CP("dve", sample_bufs["Qdiag"], src_q.rearrange("p (j e) -> p j e", e=8), [qkey, "winalias"], ["Qexp"])
                for j in range(16):
                    MM(PS4(po_)[:, h, :], Qexp[:, j, :], S0b[sb][:, j, :], False, j == 15, ["Qexp", sbk], [("ps", po_)])
                TT("pool", Vexp, rowmask, bc_mid(src_v, 16), ALU.mult, ["rowmask", vkey, "winalias"], [vxk])
                pus = [newps() for _ in range(4)]
                for b_ in range(4):
                    MM(PS(pus[b_]), kt_ap, Vexp[:, 4 * b_:4 * b_ + 4, :], True, True, [vxk, ktkey], [("ps", pus[b_])])
                for b_ in range(4):
                    TT("dve", S0f[sb][:, 4 * b_:4 * b_ + 4, :], S0f[sb][:, 4 * b_:4 * b_ + 4, :], PS4(pus[b_]), ALU.add,
                       [sfk, ("ps", pus[b_])], [sfk])
                rel(*pus)
                if evec is None:
                    ACT(S0f[sb], S0f[sb], AF.Copy, [sfk], [sfk], scale=GAM[h] ** 8)
                else:
                    TT("dve", S0f[sb], S0f[sb], bc_last(evec, 128), ALU.mult, [sfk, K("B1")], [sfk])
                return DMA("sp", state_out[:, h, :, :].rearrange("j p v -> p j v"), S0f[sb], r=[sfk])

            def ret_rec():
              po = newps()
              for h in range(4):
                MM(PS4(po)[:, h, :], ATm[:, h, :], vp[:, h, :], True, False, [K("ATm"), K("vp")], [("ps", po)])
                if kind == 0:
                    MM(PS4(po)[:, h, :], qkT[:, h, :], Sbr[:, h, :], False, True, [K("qkT"), "Sbr"], [("ps", po)])
                else:
                    out_dmas.append(sample_state(0, h, qkT[:, h, :], vp[:, h, :], ktm[:, h, :], sret, srs, None,
                                                 K("qkT"), K("vp"), K("ktm"), po))
              if kind == 0:
                pu = newps()
                for h in range(4):
                    MM(PS4(pu)[:, h, :], ktm[:, h, :], vp[:, h, :], True, True, [K("ktm"), K("vp")], [("ps", pu)])
                TT("dve", Sr, Sr, PS4(pu), ALU.add, ["Sr", ("ps", pu)], ["Sr"])
                rel(pu)
                TT("dve", Sbr, Sr, bc_last(gtab[:, 0, 2, :], 128), ALU.mult, ["Sr", "gtab"], ["Sbr"])
                TT("pool", Sr, Sr, bc_last(gtab[:, 0, 2, :], 128), ALU.mult, ["Sr", "gtab"], ["Sr"])
                if i == 15:
                    out_dmas.append(DMA("sp", srp.rearrange("h p v -> p h v"), Sr, r=["Sr"]))
              out_norm(po, gate, K("gate"), og, K("og"), gtab[:, kind, 0, :], 8 + 8 * p, gtab[:, kind, 3, :])
              rel(po)
              pt2 = newps()
              for h in range(4):
                TR(PST(pt2)[:, h, :], og[:, h, :], [K("og")], [("ps", pt2)])
              CP("act", ogT[:, 0:4, i * 128:(i + 1) * 128], PST(pt2)[:, 0:4, :], [("ps", pt2)], [("ogTa", i)])
              rel(pt2)

            def out_norm(po_, gate_, gkey, og_, ogkey, gscale, c0, gscale2=None):
                ssk = ("ss4", c0)
                ACT(og_, PS4(po_), AF.Square, [("ps", po_)], [ogkey])
                S.compute("dve", lambda e: e.tensor_reduce(out=sm[:, c0:c0 + 4], in_=og_, axis=mybir.AxisListType.X, op=ALU.add),
                          [ogkey], [ssk], cost=0.65)
                if gscale is not None:
                    TT("dve", sm[:, c0:c0 + 4], sm[:, c0:c0 + 4], gscale2, ALU.mult, [ssk, "gtab"], [ssk])
                rstd_from_ss(sm[:, c0:c0 + 4], 128, ssk)
                if gscale is not None:
                    TT("dve", sm[:, c0:c0 + 4], sm[:, c0:c0 + 4], gscale, ALU.mult, [ssk, "gtab"], [ssk])
                for h in range(4):
                    STT(og_[:, h, :], PS4(po_)[:, h, :], sm[:, c0 + h:c0 + h + 1], gate_[:, h * 128:(h + 1) * 128],
                        ALU.mult, ALU.mult, [("ps", po_), ssk, gkey], [ogkey])

            def hg_prep():
              for cb in (6, 7):
                  pb = newps()
                  pr[cb] = pb
                  for kc in range(8):
                      MM(PS(pb), hT[:, kc, :], win[:, kc, cb * 512:(cb + 1) * 512], kc == 0, kc == 7,
                         [("hT", i % NHT), ("win", cb)], [("ps", pb)])
              for cb in (4, 5):
                  pb = newps()
                  pr[cb] = pb
                  for h in range(4):
                      for kc in range(8):
                          MM(PS4(pb)[:, h, :], win[:, kc, cb * 512 + h * 128:cb * 512 + (h + 1) * 128], hT[:, kc, :],
                             kc == 0, kc == 7, [("hT", i % NHT), ("win", cb)], [("ps", pb)])
              ACT(B1, PS(pr[5]), AF.Tanh, [("ps", pr[5])], [K("B1")], scale=0.5)
              rel(pr[5])
              TS("dve", B3, B1, -1.0, 1.0, ALU.mult, ALU.add, [K("B1")], [K("B3")])
              for h in range(4):
                  ACT(B1h[:, h, :], B1h[:, h, :], AF.Ln, [K("B1"), "lbh", "homl"], [K("B1")], scale=homl[:, h:h + 1], bias=lbh[:, h:h + 1])
              S.compute("dve", lambda e, k_=kind: e.tensor_tensor_scan(out=B2, data0=scanm[:, k_, :], data1=B1, initial=0.0,
                                                                       op0=ALU.mult, op1=ALU.add),
                        [K("B1"), "scanm"], [K("B2")], cost=1.2)
              ACT(B1, B2, AF.Exp, [K("B2")], [K("B1")])
              ACT(B2, B2, AF.Exp, [K("B2")], [K("B2")], scale=-1.0)
              ACT(B4, PS(pr[4]), AF.Silu, [("ps", pr[4])], [K("B4")])
              rel(pr[4])
              STT(qtT.rearrange("p h t -> p (h t)"), B4, SCALE, B1, ALU.mult, ALU.mult, [K("B4"), K("B1")], [K("qtT")])
              for h in range(4):
                  STT(ktT[:, h, :], B3h[:, h, :], homl[:, h:h + 1], B2h[:, h, :], ALU.mult, ALU.mult,
                      [K("B3"), K("B2"), "homl"], [("PART", K("ktT"), h)])
              pk2 = newps()
              for h in range(4):
                  TR(PST(pk2)[:, h, :], ktT[:, h, :], [K("ktT")], [("ps", pk2)])
              CP("dve", ktmh, PST(pk2)[:, 0:4, :], [("ps", pk2)], [K("ktmh")])
              CP("act", vph, PS4(pr[6]), [("ps", pr[6])], [K("vph")])
              ACT(gateh, PS(pr[7]), AF.Silu, [("ps", pr[7])], [K("gateh")])
              rel(pk2, pr[6], pr[7])
              TT("pool", gateh, gateh, hnw, ALU.mult, [K("gateh"), "hnw"], [K("gateh")])
              pa2 = newps()
              for h in range(4):
                  MM(PS4(pa2)[:, h, :], ktT[:, h, :], qtT[:, h, :], True, True, [K("ktT"), K("qtT")], [("ps", pa2)])
              mk = 1 if kind == 0 else 2
              TT("dve", ATh, PS4(pa2), bc_mid(masks[:, mk, :], 4), ALU.mult, [("ps", pa2), "masks"], [K("ATmh")])
              rel(pa2)

            def hg_rec():
              po2 = newps()
              if kind == 0:
                  ca = hchunk
                  prev = (ca + 1) % 2
                  cur = ca % 2
                  pu1 = newps()
                  for h in range(4):
                      MM(PS4(pu1)[:, h, :], ktmh[0:64, h, :], vph[0:64, h, :], True, True, [K("ktmh"), K("vph")], [("ps", pu1)])
                  TT("dve", Sh, Sh, PS4(pu1), ALU.add, ["Sh", ("ps", pu1)], ["Sh"])
                  rel(pu1)
                  TT("dve", Sbh[cur], Sh, B1h[:, :, 63:64].broadcast_to([128, 4, 128]), ALU.mult, ["Sh", K("B1")], [("Sbh", cur)])
                  TT("pool", Sh, Sh, B1h[:, :, 63:64].broadcast_to([128, 4, 128]), ALU.mult, ["Sh", K("B1")], ["Sh"])
                  for h in range(4):
                      MM(PS4(po2)[:, h, :], ATh[:, h, :], vph[:, h, :], True, False, [K("ATmh"), K("vph")], [("ps", po2)])
                      MM(PS4(po2)[0:64, h, :], qtT[:, h, 0:64], Sbh[prev][:, h, :], False, True, [K("qtT"), ("Sbh", prev)], [("ps", po2)])
                      MM(PS4(po2)[64:128, h, :], qtT[:, h, 64:128], Sbh[cur][:, h, :], False, True, [K("qtT"), ("Sbh", cur)], [("ps", po2)])
                  pu2 = newps()
                  for h in range(4):
                      MM(PS4(pu2)[:, h, :], ktmh[64:128, h, :], vph[64:128, h, :], True, True, [K("ktmh"), K("vph")], [("ps", pu2)])
                  TT("dve", Sh, Sh, PS4(pu2), ALU.add, ["Sh", ("ps", pu2)], ["Sh"])
                  rel(pu2)
                  TT("dve", Sbh[prev], Sh, B1h[:, :, 127:128].broadcast_to([128, 4, 128]), ALU.mult, ["Sh", K("B1")], [("Sbh", prev)])
                  TT("pool", Sh, Sh, B1h[:, :, 127:128].broadcast_to([128, 4, 128]), ALU.mult, ["Sh", K("B1")], ["Sh"])
                  if i == 15:
                      out_dmas.append(DMA("sp", shp.rearrange("h p v -> p h v"), Sh, r=["Sh"]))
              else:
                  for h in range(4):
                      MM(PS4(po2)[:, h, :], ATh[:, h, :], vph[:, h, :], True, False, [K("ATmh"), K("vph")], [("ps", po2)])
                      evec = B1h[:, h, :].rearrange("p (j e) -> p j e", e=8)[:, :, 7]
                      out_dmas.append(sample_state(1, h, qtT[:, h, :], vph[:, h, :], ktmh[:, h, :], shg, shs, evec,
                                                   K("qtT"), K("vph"), K("ktmh"), po2))
              out_norm(po2, gateh, K("gateh"), ogh, K("ogh"), None, 24 + 8 * p)
              rel(po2)
              pt3 = newps()
              for h in range(4):
                  TR(PST(pt3)[:, h, :], ogh[:, h, :], [K("ogh")], [("ps", pt3)])
              CP("act", ogT[:, 4:8, i * 128:(i + 1) * 128], PST(pt3)[:, 0:4, :], [("ps", pt3)], [("ogTb", i)])
              rel(pt3)

            if kind == 0:
                ret_rec()
                hg_prep()
                hg_rec()
            else:
                hg_prep()
                S.compute("pe", lambda e: e.ldweights(ident), ["ident"], [("win", cb_) for cb_ in range(8)] + ["winalias"], cost=0.05)
                DMA("sp", sample_bufs["rowmask"], rowmask_d, r=["winalias"], w=["rowmask"])
                S.compute("pool", lambda e: e.memset(sample_bufs["Qexp"], 0.0), ["winalias"], ["Qexp"], cost=4.0)
                ret_rec()
                hg_rec()

        hchunk = 0
        import os
        DBG = int(os.environ.get("DBG_TILES", "0"))
        for i in range(DBG if DBG else 16):
            mixer_tile(i, hchunk)
            hchunk += 2
        if DBG:
            S.final_wait("sp", out_dmas)
            S.emit()
            return nc
        A2 = Arena(arena_t[:, :], NB)
        A2.off = mark_win
        sample_bufs["rowmask"] = A2.alloc([16, 128], BF16)
        sample_bufs["Qexp"] = A2.alloc([16, 128], BF16)
        sample_bufs["Vexp"] = [A2.alloc([16, 128], BF16) for _ in range(2)]
        sample_bufs["S0f"] = [A2.alloc([16, 128], F32) for _ in range(4)]
        sample_bufs["S0b"] = [A2.alloc([16, 128], BF16) for _ in range(2)]
        assert A2.off <= mark_win + 8 * 4096 * 2
        qx = sample_bufs["Qexp"]
        sample_bufs["Qdiag"] = bass.AP(qx.tensor, qx.offset, [[qx.ap[0][0], 128], [136, 16], [1, 8]])
        mixer_tile(16, hchunk)
        S.barrier()

        A.off = mark_win
        aT = A.alloc([32, 512], BF16)
        Xbs = [A.alloc([4, D], F32) for _ in range(2)]
        h2Ts = [A.alloc([8, 512], BF16) for _ in range(2)]
        nffn = A.alloc([D], F32)
        nfin = A.alloc([D], F32)
        wout = A.alloc([8, D], BF16)
        wupb = [A.alloc([8, 512], BF16) for _ in range(2)]
        wdnb = [A.alloc([4, D], BF16) for _ in range(2)]
        wdn4 = [wdnb[j_ // 2][:, :, (j_ % 2) * 512:(j_ % 2 + 1) * 512] for j_ in range(4)]
        h2s = [A.alloc([D], BF16) for _ in range(2)]
        rl = [A.alloc([512], F32) for _ in range(2)]
        print("post arena bytes/partition:", A.off)

        DMA("sp", nffn, nffn_d, w=["nffn"])
        DMA("sp", nfin, nfin_d, w=["nfin"])
        DMA("sp", wout, wo_s, r=["wo_s"], w=["wout"])

        blocks = [[0, 1, 2, 3], [4, 5, 6], [7, 8, 9], [10, 11, 12], [13, 14, 15, 16]]
        ydmas = []

        def prologue(bi):
            blk = blocks[bi]
            bp = bi % 2
            Xb = Xbs[bp]
            h2T = h2Ts[bp]
            for l, i in enumerate(blk):
                xk = ("Xb", bp, l)
                h2 = h2s[l % 2]
                hk = ("h2", l % 2)
                c2 = 2 + (l % 2)
                DMA("sp", Xb[:, l, :], x[i], w=[xk])
                pm = [newps(), newps()]
                for half in range(2):
                    for mc in range(8):
                        MM(PS(pm[half]), ogT[:, mc, i * 128:(i + 1) * 128], wout[:, mc, half * 512:(half + 1) * 512],
                           mc == 0, mc == 7, [("ogTa", i), ("ogTb", i), "wout"], [("ps", pm[half])])
                for half in range(2):
                    TT("dve", Xb[:, l, half * 512:(half + 1) * 512], Xb[:, l, half * 512:(half + 1) * 512], PS(pm[half]), ALU.add,
                       [xk, ("ps", pm[half])], [xk])
                rel(*pm)
                ACT(h2, Xb[:, l, :], AF.Square, [xk], [hk, ("ss2", c2)], accum=sm[:, c2:c2 + 1])
                rstd_from_ss(sm[:, c2:c2 + 1], D, ("ss2", c2))
                STT(h2, Xb[:, l, :], sm[:, c2:c2 + 1], nffn, ALU.mult, ALU.mult, [xk, ("ss2", c2), "nffn"], [hk])
                pt = newps()
                for kc in range(8):
                    TR(PST(pt)[:, kc, :], h2[:, kc * 128:(kc + 1) * 128], [hk], [("ps", pt)])
                CP("act", h2T[:, :, l * 128:(l + 1) * 128], PST(pt), [("ps", pt)], [("h2T", bp, l)])
                rel(pt)

        def up_phase(bi):
            blk = blocks[bi]
            nl = len(blk)
            ntok = nl * 128
            bp = bi % 2
            h2T = h2Ts[bp]
            h2keys = [("h2T", bp, l) for l in range(nl)]
            for cb in range(8):
                wb = wupb[cb % 2]
                wk = ("wupb", cb % 2)
                DMA("sp", wb, wu_s[cb], r=[("wu_s", cb)], w=[wk])
                for f in range(4):
                    ffc = cb * 4 + f
                    pb = newps()
                    for kc in range(8):
                        MM(PS(pb)[:, 0:ntok], wb[:, kc, f * 128:(f + 1) * 128], h2T[:, kc, 0:ntok], kc == 0, kc == 7,
                           h2keys + [wk], [("ps", pb)])
                    rb = rl[ffc % 2]
                    rk = ("rl", ffc % 2)
                    ACT(rb[:, 0:ntok], PS(pb)[:, 0:ntok], AF.Relu, [("ps", pb)], [rk])
                    rel(pb)
                    TT("pool", aT[:, ffc, 0:ntok], rb[:, 0:ntok], rb[:, 0:ntok], ALU.mult, [rk], [("aT", ffc)])

        def down_phase(bi):
            blk = blocks[bi]
            nl = len(blk)
            bp = bi % 2
            Xb = Xbs[bp]
            for half in range(2):
                banks = [newps() for _ in range(nl)]
                for q in range(8):
                    wi = (2 * bi * 8 + half * 8 + q) % 4
                    wb = wdn4[wi]
                    wk = ("wdn4", wi)
                    DMA("sp", wb, wd_s[q][:, :, half * 512:(half + 1) * 512], r=[("wd_s", q)], w=[wk])
                    for f in range(4):
                        ffc = q * 4 + f
                        for l in range(nl):
                            MM(PS(banks[l]), aT[:, ffc, l * 128:(l + 1) * 128], wb[:, f, :],
                               ffc == 0, ffc == 31, [("aT", ffc), wk], [("ps", banks[l])])
                for l in range(nl):
                    xk = ("Xb", bp, l)
                    TT("dve", Xb[:, l, half * 512:(half + 1) * 512], Xb[:, l, half * 512:(half + 1) * 512], PS(banks[l]),
                       ALU.add, [xk, ("ps", banks[l])], [xk])
                    rel(banks[l])
            for l, i in enumerate(blk):
                xk = ("Xb", bp, l)
                c3 = 4 + (l % 2)
                ACT(h2s[l % 2], Xb[:, l, :], AF.Square, [xk], [("h2", l % 2), ("ss3", c3)], accum=sm[:, c3:c3 + 1])
                rstd_from_ss(sm[:, c3:c3 + 1], D, ("ss3", c3))
                STT(Xb[:, l, :], Xb[:, l, :], sm[:, c3:c3 + 1], nfin, ALU.mult, ALU.mult, [xk, ("ss3", c3), "nfin"], [xk])
                ydmas.append(DMA("sp", y[i], Xb[:, l, :], r=[xk]))

        prologue(0)
        for bi in range(len(blocks)):
            up_phase(bi)
            if bi + 1 < len(blocks):
                prologue(bi + 1)
            down_phase(bi)

        S.final_wait("sp", ydmas + out_dmas)
        S.emit()
    return nc


def _consts():
    f32 = np.float32
    half = 64
    inv_freq = (f32(10000.0) ** (-(np.arange(half, dtype=f32)) / f32(half))).astype(f32)
    pos = np.zeros((128, NT), dtype=f32)
    for i in range(16):
        pos[:, i] = i * 128 + np.arange(128)
    pos[:, 16] = 16384 + (np.arange(128) % 8)
    ang = (pos[:, :, None] * inv_freq[None, None, :]).astype(f32)
    cs = np.stack([np.cos(ang).astype(f32), np.sin(ang).astype(f32)], axis=1)
    gam = np.array(GAM, dtype=np.float64)
    lg = np.log(gam)
    gtab = np.zeros((128, 2, 4, 4), dtype=f32)
    p = np.arange(128)
    for kind, j in ((0, p), (1, p % 8)):
        gtab[:, kind, 0, :] = np.exp((j[:, None] + 1) * lg[None, :])
        gtab[:, kind, 1, :] = np.exp(-(j[:, None] + 1) * lg[None, :]) * SCALE
        gtab[:, kind, 3, :] = np.exp(2 * (j[:, None] + 1) * lg[None, :])
    gtab[:, 0, 2, :] = np.exp(128 * lg)[None, :]
    gtab[:, 1, 2, :] = np.exp(8 * lg)[None, :]
    s = p[:, None]
    t = p[None, :]
    masks = np.zeros((128, 3, 128), dtype=f32)
    masks[:, 0, :] = (s <= t)
    masks[:, 1, :] = (s <= t) & (s // 64 == t // 64)
    masks[:, 2, :] = (s <= t) & (s // 8 == t // 8)
    scanm = np.ones((128, 2, 512), dtype=f32)
    scanm[:, 0, ::64] = 0.0
    scanm[:, 1, ::8] = 0.0
    colmask = np.zeros((128, 16, 128), dtype=f32)
    rowmask = np.zeros((128, 16, 128), dtype=f32)
    for j in range(16):
        colmask[:, j, 8 * j:8 * j + 8] = 1.0
        rowmask[8 * j:8 * j + 8, j, :] = 1.0
    bf = ml_dtypes.bfloat16
    return {
        "ident": np.eye(128, dtype=f32).astype(bf),
        "cs": np.ascontiguousarray(cs),
        "gtab": gtab,
        "masks": masks.astype(bf),
        "scanm": scanm,
        "colmask": colmask.astype(bf),
        "rowmask": rowmask.astype(bf),
    }


_NC_CACHE = {}


def kernel(x_prompt, x_sample, state_ret, state_hgrn, norm_mix_w, w_in, ret_norm_w, hgrn_norm_w,
           lb_logits, w_out, norm_ffn_w, w_up, w_down, final_norm_w):
    f32 = np.float32
    asf = lambda a: np.ascontiguousarray(np.asarray(a, dtype=f32))
    x_prompt, x_sample = asf(x_prompt), asf(x_sample)
    state_ret, state_hgrn = asf(state_ret), asf(state_hgrn)
    if "nc" not in _NC_CACHE:
        for w_ in (60, 30, 20, 10, 4, 0):
            try:
                _NC_CACHE["nc"] = build_nc(window=w_)
                break
            except RuntimeError as ex_:
                if "scheduler deadlock" not in str(ex_):
                    raise
    nc = _NC_CACHE["nc"]
    cst = _consts()
    bc = lambda v, n: np.ascontiguousarray(np.broadcast_to(asf(v).reshape(1, -1), (128, n)))
    shared = {
        "w_in": asf(w_in)[0], "w_out": asf(w_out)[0], "w_up": asf(w_up)[0], "w_down": asf(w_down)[0],
        "nmix": bc(norm_mix_w[0], D), "nffn": bc(norm_ffn_w[0], D), "nfin": bc(final_norm_w, D),
        "rnw": bc(np.tile(asf(ret_norm_w)[0], 4), 512), "hnw": bc(np.tile(asf(hgrn_norm_w)[0], 4), 512),
        "lbl": np.ascontiguousarray(asf(lb_logits).reshape(2, 4, 128).transpose(2, 0, 1)),
    }
    shared.update(cst)
    in_maps = []
    for c in range(NCORES):
        xc = np.concatenate([x_prompt[c], x_sample[16 * c:16 * c + 16].reshape(128, D)], axis=0).reshape(NT, 128, D)
        m = dict(shared)
        m["x"] = np.ascontiguousarray(xc)
        m["sret"] = np.ascontiguousarray(state_ret[0, 16 * c:16 * c + 16])
        m["shg"] = np.ascontiguousarray(state_hgrn[0, 16 * c:16 * c + 16])
        in_maps.append(m)
    res = run_bass_kernel_spmd(nc, in_maps, core_ids=list(range(NCORES)))
    rs = res.results
    yall = np.stack([r["y"].reshape(NT * 128, D) for r in rs], axis=0)
    y_prompt = np.ascontiguousarray(yall[:, :2048, :])
    y_sample = np.ascontiguousarray(yall[:, 2048:, :].reshape(128, 8, D))
    ret_p = np.stack([r["srp"] for r in rs], axis=0)[None]
    hg_p = np.stack([r["shp"] for r in rs], axis=0)[None]
    ret_s = np.concatenate([r["srs"] for r in rs], axis=0)[None]
    hg_s = np.concatenate([r["shs"] for r in rs], axis=0)[None]
    return (y_prompt.astype(f32), y_sample.astype(f32), ret_p.astype(f32), hg_p.astype(f32),
            ret_s.astype(f32), hg_s.astype(f32))
```
